# Optimizing a Trainium2 kernel written in Bass

```python
import math
import jax, jax.numpy as jnp
from jax import lax
import numpy as np

D_MODEL = 2048
BATCH = 2
SEQ = 8192
DEPTH = 2
DEC_BATCH = 16
DEC_SEQ = 2048
PAST_LEN = 128

D_RNN = D_MODEL // 2
N_BLOCKS = 16
BLOCK_W = D_RNN // N_BLOCKS
CONV_RG = 4
RG_C = 8.0
H_GLA = 4
DK_TOT = D_MODEL // 4
DV_TOT = D_MODEL // 2
DK_HEAD = DK_TOT // H_GLA
DV_HEAD = DV_TOT // H_GLA
GATE_RANK = 16
GATE_NORM = 16.0
CHUNK = 64
D_FF = 3 * D_MODEL
CONV_FF = 3
D_IN = 2 * D_RNN + 2 * DK_TOT + 2 * DV_TOT + 2 * GATE_RANK + 2 * D_MODEL
ALPHA = (2.0 * DEPTH) ** 0.25
BETA = (8.0 * DEPTH) ** -0.25
EPS = 1e-5

kernel_name = "hybrid_rglru_gla_convffn_encoder"


def _layer_norm(x, g, b):
    xf = x.astype(jnp.float32)
    mu = jnp.mean(xf, axis=-1, keepdims=True)
    var = jnp.mean(jnp.square(xf - mu), axis=-1, keepdims=True)
    return ((xf - mu) * lax.rsqrt(var + EPS) * g + b).astype(x.dtype)


def _dwconv(x, w, b, pad_lo, pad_hi):
    s = x.shape[1]
    xp = jnp.pad(x, ((0, 0), (pad_lo, pad_hi), (0, 0)))
    out = b
    for j in range(w.shape[0]):
        out = out + w[j] * xp[:, j:j + s]
    return out


def _lin_combine(c1, c2):
    a1, b1 = c1
    a2, b2 = c2
    return a1 * a2, a2 * b1 + b2


def _rglru_dir(xc, wa, ba, wx, bx, lam, reverse):
    bsz, s, w = xc.shape
    xb = xc.reshape(bsz, s, N_BLOCKS, BLOCK_W)
    r = jax.nn.sigmoid(jnp.einsum('bsni,nij->bsnj', xb, wa).reshape(bsz, s, w) + ba)
    i = jax.nn.sigmoid(jnp.einsum('bsni,nij->bsnj', xb, wx).reshape(bsz, s, w) + bx)
    log_a = -RG_C * r * jax.nn.softplus(-lam)
    a = jnp.exp(log_a)
    u = jnp.sqrt(-jnp.expm1(2.0 * log_a)) * (i * xc)
    _, h = lax.associative_scan(_lin_combine, (a, u), axis=1, reverse=reverse)
    return h


def _gla_dir(q, k, v, g):
    bsz, h, s, dk = q.shape
    dv = v.shape[-1]
    n = s // CHUNK
    q = q.reshape(bsz, h, n, CHUNK, dk)
    k = k.reshape(bsz, h, n, CHUNK, dk)
    v = v.reshape(bsz, h, n, CHUNK, dv)
    gc = jnp.cumsum(g.reshape(bsz, h, n, CHUNK, dk), axis=3)
    g_last = gc[:, :, :, -1]
    q_d = q * jnp.exp(gc)
    k_d = k * jnp.exp(-gc)
    mask = jnp.tril(jnp.ones((CHUNK, CHUNK), dtype=bool))
    att = jnp.einsum('bhncd,bhnsd->bhncs', q_d, k_d)
    att = jnp.where(mask, att, 0.0)
    o_intra = jnp.einsum('bhncs,bhnsv->bhncv', att, v)
    k_end = k * jnp.exp(g_last[:, :, :, None, :] - gc)
    kv = jnp.einsum('bhncd,bhncv->bhndv', k_end, v)
    dec = jnp.exp(g_last)

    def step(state, inp):
        d_n, kv_n = inp
        return d_n[..., None] * state + kv_n, state

    init = jnp.zeros((bsz, h, dk, dv), dtype=q.dtype)
    _, s_prev = lax.scan(step, init, (jnp.moveaxis(dec, 2, 0), jnp.moveaxis(kv, 2, 0)))
    s_prev = jnp.moveaxis(s_prev, 0, 2)
    o_inter = jnp.einsum('bhncd,bhndv->bhncv', q_d, s_prev)
    return (o_intra + o_inter).reshape(bsz, h, s, dv)


def _token_mixer(x, w_in, b_in, conv_rg_w, conv_rg_b, rg_wa, rg_ba, rg_wx, rg_bx, rg_lam,
                 gla_wg2, gla_bg, gla_norm_w, w_proj_a, w_proj_b, w_out):
    bsz, s, _ = x.shape
    z = x @ w_in + b_in
    sizes = [D_RNN, D_RNN, DK_TOT, DK_TOT, DV_TOT, DV_TOT, GATE_RANK, GATE_RANK, D_MODEL, D_MODEL]
    offs = [int(o) for o in np.cumsum(sizes)[:-1]]
    z_rx, z_rg, z_q, z_k, z_v, z_og, z_gf, z_gb, z_ma, z_mb = jnp.split(z, offs, axis=-1)

    xa = _dwconv(z_rx, conv_rg_w, conv_rg_b, CONV_RG // 2, CONV_RG - 1 - CONV_RG // 2).astype(jnp.float32)
    h_a = (_rglru_dir(xa, rg_wa[0], rg_ba[0], rg_wx[0], rg_bx[0], rg_lam[0], False)
           + _rglru_dir(xa, rg_wa[1], rg_ba[1], rg_wx[1], rg_bx[1], rg_lam[1], True))
    y_a = (jax.nn.gelu(z_rg) * h_a.astype(x.dtype)) @ w_proj_a

    def heads(t, dh):
        return t.reshape(bsz, s, H_GLA, dh).transpose(0, 2, 1, 3).astype(jnp.float32)

    q = heads(z_q, DK_HEAD) * (DK_HEAD ** -0.5)
    k = heads(z_k, DK_HEAD)
    v = heads(z_v, DV_HEAD)
    g_f = heads(jax.nn.log_sigmoid(z_gf @ gla_wg2[0] + gla_bg[0]) / GATE_NORM, DK_HEAD)
    g_b = heads(jax.nn.log_sigmoid(z_gb @ gla_wg2[1] + gla_bg[1]) / GATE_NORM, DK_HEAD)
    o = _gla_dir(q, k, v, g_f) + jnp.flip(
        _gla_dir(jnp.flip(q, 2), jnp.flip(k, 2), jnp.flip(v, 2), jnp.flip(g_b, 2)), 2)
    o = o * lax.rsqrt(jnp.mean(jnp.square(o), axis=-1, keepdims=True) + EPS) * gla_norm_w
    o = o.transpose(0, 2, 1, 3).reshape(bsz, s, DV_TOT).astype(x.dtype)
    y_b = (o * jax.nn.silu(z_og)) @ w_proj_b

    merged = jax.nn.sigmoid(z_ma) * y_a + jax.nn.sigmoid(z_mb) * y_b
    return merged @ w_out


def _conv_ffn(x, w_up, conv_ff_w, conv_ff_b, w_down):
    u = x @ w_up
    u_g, u_v = jnp.split(u, 2, axis=-1)
    hdn = jax.nn.gelu(_dwconv(u_g, conv_ff_w, conv_ff_b, CONV_FF // 2, CONV_FF // 2)) * u_v
    return hdn @ w_down


def _trunk(x, ln_in_g, ln_in_b, w_in, b_in, conv_rg_w, conv_rg_b, rg_wa, rg_ba, rg_wx, rg_bx,
           rg_lam, gla_wg2, gla_bg, gla_norm_w, w_proj_a, w_proj_b, w_out, ln_mix_g, ln_mix_b,
           w_up, conv_ff_w, conv_ff_b, w_down, ln_ffn_g, ln_ffn_b):
    x = _layer_norm(x, ln_in_g, ln_in_b)
    for l in range(DEPTH):
        mix = _token_mixer(x, w_in[l], b_in[l], conv_rg_w[l], conv_rg_b[l], rg_wa[l], rg_ba[l],
                           rg_wx[l], rg_bx[l], rg_lam[l], gla_wg2[l], gla_bg[l], gla_norm_w[l],
                           w_proj_a[l], w_proj_b[l], w_out[l])
        x = _layer_norm(ALPHA * x + mix, ln_mix_g[l], ln_mix_b[l])
        ff = _conv_ffn(x, w_up[l], conv_ff_w[l], conv_ff_b[l], w_down[l])
        x = _layer_norm(ALPHA * x + ff, ln_ffn_g[l], ln_ffn_b[l])
    return x


def setup_inputs(seed: int = 0) -> dict:
    key = jax.random.key(seed)
    ks = jax.random.split(key, 32)
    f32 = jnp.float32

    def nrm(k, shape, scale):
        return jax.random.normal(k, shape, dtype=f32) * scale

    a0 = jax.random.uniform(ks[10], (DEPTH, 2, D_RNN), dtype=f32, minval=0.9, maxval=0.999)
    return {
        "x_prompt": nrm(ks[0], (BATCH, SEQ, D_MODEL), 1.0),
        "x_sample": nrm(ks[1], (DEC_BATCH, DEC_SEQ, D_MODEL), 1.0),
        "ln_in_g": 1.0 + nrm(ks[2], (D_MODEL,), 0.02),
        "ln_in_b": nrm(ks[3], (D_MODEL,), 0.02),
        "w_in": nrm(ks[4], (DEPTH, D_MODEL, D_IN), D_MODEL ** -0.5),
        "b_in": nrm(ks[5], (DEPTH, D_IN), 0.02),
        "conv_rg_w": nrm(ks[6], (DEPTH, CONV_RG, D_RNN), CONV_RG ** -0.5),
        "conv_rg_b": nrm(ks[7], (DEPTH, D_RNN), 0.02),
        "rg_wa": nrm(ks[8], (DEPTH, 2, N_BLOCKS, BLOCK_W, BLOCK_W), BLOCK_W ** -0.5),
        "rg_ba": nrm(ks[9], (DEPTH, 2, D_RNN), 0.02),
        "rg_wx": nrm(ks[11], (DEPTH, 2, N_BLOCKS, BLOCK_W, BLOCK_W), BLOCK_W ** -0.5),
        "rg_bx": nrm(ks[12], (DEPTH, 2, D_RNN), 0.02),
        "rg_lam": jnp.log(a0) - jnp.log1p(-a0),
        "gla_wg2": nrm(ks[13], (DEPTH, 2, GATE_RANK, DK_TOT), GATE_RANK ** -0.5),
        "gla_bg": nrm(ks[14], (DEPTH, 2, DK_TOT), 0.1),
        "gla_norm_w": 1.0 + nrm(ks[15], (DEPTH, DV_HEAD), 0.02),
        "w_proj_a": nrm(ks[16], (DEPTH, D_RNN, D_MODEL), BETA * D_RNN ** -0.5),
        "w_proj_b": nrm(ks[17], (DEPTH, DV_TOT, D_MODEL), BETA * DV_TOT ** -0.5),
        "w_out": nrm(ks[18], (DEPTH, D_MODEL, D_MODEL), BETA * D_MODEL ** -0.5),
        "ln_mix_g": 1.0 + nrm(ks[19], (DEPTH, D_MODEL), 0.02),
        "ln_mix_b": nrm(ks[20], (DEPTH, D_MODEL), 0.02),
        "w_up": nrm(ks[21], (DEPTH, D_MODEL, 2 * D_FF), D_MODEL ** -0.5),
        "conv_ff_w": nrm(ks[22], (DEPTH, CONV_FF, D_FF), CONV_FF ** -0.5),
        "conv_ff_b": nrm(ks[23], (DEPTH, D_FF), 0.02),
        "w_down": nrm(ks[24], (DEPTH, D_FF, D_MODEL), BETA * D_FF ** -0.5),
        "ln_ffn_g": 1.0 + nrm(ks[25], (DEPTH, D_MODEL), 0.02),
        "ln_ffn_b": nrm(ks[26], (DEPTH, D_MODEL), 0.02),
    }


def reference(x_prompt, x_sample, ln_in_g, ln_in_b, w_in, b_in, conv_rg_w, conv_rg_b, rg_wa,
              rg_ba, rg_wx, rg_bx, rg_lam, gla_wg2, gla_bg, gla_norm_w, w_proj_a, w_proj_b,
              w_out, ln_mix_g, ln_mix_b, w_up, conv_ff_w, conv_ff_b, w_down, ln_ffn_g, ln_ffn_b):
    y_prompt = _trunk(x_prompt, ln_in_g, ln_in_b, w_in, b_in, conv_rg_w, conv_rg_b, rg_wa, rg_ba,
                      rg_wx, rg_bx, rg_lam, gla_wg2, gla_bg, gla_norm_w, w_proj_a, w_proj_b, w_out,
                      ln_mix_g, ln_mix_b, w_up, conv_ff_w, conv_ff_b, w_down, ln_ffn_g, ln_ffn_b)
    y_sample = _trunk(x_sample, ln_in_g, ln_in_b, w_in, b_in, conv_rg_w, conv_rg_b, rg_wa, rg_ba,
                      rg_wx, rg_bx, rg_lam, gla_wg2, gla_bg, gla_norm_w, w_proj_a, w_proj_b, w_out,
                      ln_mix_g, ln_mix_b, w_up, conv_ff_w, conv_ff_b, w_down, ln_ffn_g, ln_ffn_b)
    return (y_prompt, y_sample)
```

```python
import numpy as np
from contextlib import ExitStack
import concourse.bass as bass
import concourse.mybir as mybir
from concourse.bass_utils import run_bass_kernel_spmd

F32 = mybir.dt.float32
BF16 = mybir.dt.bfloat16
AF = mybir.ActivationFunctionType
ALU = mybir.AluOpType

D = 2048
DEPTH = 2
D_RNN = 1024
DK = 512
DV = 1024
D_FF = 6144
D_IN = 9248
ALPHA = float((2.0 * DEPTH) ** 0.25)
EPS = 1e-5
QSCALE = float(128 ** -0.5)
NT = 512
O_RX, O_RG, O_Q, O_K, O_V, O_OG, O_GF, O_GB, O_MA, O_MB = 0, 1024, 2048, 2560, 3072, 4096, 5120, 5136, 5152, 7200


class Buf:
    __slots__ = ("name", "w", "r", "dsem", "dtot")

    def __init__(self, name):
        self.name = name
        self.w = None
        self.r = {}
        self.dsem = None
        self.dtot = 0


class Sched:
    def __init__(self, nc, es):
        self.nc = nc
        self.es = es
        self.eng = {"pe": nc.tensor, "act": nc.scalar, "dve": nc.vector, "pool": nc.gpsimd, "sp": nc.sync}
        self.sems = {}
        self.tot = {}
        self.waited = {k: {} for k in self.eng}
        for k in ("pe", "act", "dve", "pool"):
            self.sems[k] = es.enter_context(nc.semaphore("s_" + k))
            self.tot[k] = 0
        self.nd = 0
        self.free_dsems = []
        self.dbufs = []
        self.protected = set()

    def _dsem(self, buf):
        if buf.dsem is None:
            if self.free_dsems:
                key = self.free_dsems.pop()
            else:
                key = "d%d" % self.nd
                self.nd += 1
                self.sems[key] = self.es.enter_context(self.nc.semaphore("s_" + key))
                self.tot[key] = 0
            buf.dsem = key
            self.dbufs.append(buf)
        return buf.dsem

    def _wait(self, e, ev):
        if ev is None:
            return
        key, val = ev
        if self.waited[e].get(key, 0) >= val:
            return
        assert val <= self.tot[key], (key, val, self.tot[key])
        self.eng[e].wait_ge(self.sems[key], val)
        self.waited[e][key] = val

    def _deps(self, e, reads, writes, same_eng_key):
        for b in reads:
            if b.w is not None:
                if b.w[0] == same_eng_key and e == "pe":
                    continue
                self._wait(e, b.w)
        for b in writes:
            if b.w is not None and b.w[0] != same_eng_key:
                self._wait(e, b.w)
            for k_, v_ in b.r.items():
                if k_ != same_eng_key:
                    self._wait(e, (k_, v_))

    def op(self, e, fn, reads=(), writes=(), signal=True):
        self._deps(e, reads, writes, e)
        ins = fn()
        if signal:
            self.tot[e] += 1
            ins.then_inc(self.sems[e], 1)
            ev = (e, self.tot[e])
        else:
            ev = (e, self.tot[e] + 1)
        for b in reads:
            if b.r.get(ev[0], 0) < ev[1]:
                b.r[ev[0]] = ev[1]
        for b in writes:
            b.w = ev
            b.r = {}
        return ins

    def dma(self, q, out, in_, sb, reads=(), writes=(), **kw):
        key = self._dsem(sb)
        for b in writes:
            if b.w is not None and b.w[0] == key:
                b.w = None
        self._deps(q, reads, writes, None)
        ins = self.eng[q].dma_start(out=out, in_=in_, **kw)
        self.tot[key] += 16
        ins.then_inc(self.sems[key], 16)
        ev = (key, self.tot[key])
        for b in reads:
            b.r[key] = ev[1]
        for b in writes:
            b.w = ev
            b.r = {}
        return ins

    def barrier(self):
        prot = set(b.dsem for b in self.protected if b.dsem is not None)
        for e in self.eng:
            for key, v in self.tot.items():
                if v > 0 and key not in prot:
                    self._wait(e, (key, v))
        keep = []
        for b in self.dbufs:
            if b in self.protected:
                keep.append(b)
                continue
            b.dsem = None
            b.w = None
            b.r = {}
        self.dbufs = keep
        self.free_dsems = [k for k in self.sems if k.startswith("d") and k not in prot]


def build_program(NS, SL, final_only=True):
    T = NS * SL
    NTILE = T // NT
    TPS = SL // NT
    NCH = T // 128
    nc = bass.Bass("TRN2", target_bir_lowering=False)

    def din(name, shape):
        return nc.dram_tensor(name, list(shape), F32, kind="ExternalInput").ap()

    x_in = din("x", (T, D))
    flags_in = din("flags", (128, 4))
    ln_in_g = din("ln_in_g", (D,)); ln_in_b = din("ln_in_b", (D,))
    w_in = din("w_in", (DEPTH, D, D_IN)); b_in = din("b_in", (DEPTH, D_IN))
    conv_rg_w = din("conv_rg_w", (DEPTH, 4, D_RNN)); conv_rg_b = din("conv_rg_b", (DEPTH, D_RNN))
    rg_wa = din("rg_wa", (DEPTH, 2, 16, 64, 64)); rg_ba = din("rg_ba", (DEPTH, 2, D_RNN))
    rg_wx = din("rg_wx", (DEPTH, 2, 16, 64, 64)); rg_bx = din("rg_bx", (DEPTH, 2, D_RNN))
    rg_lam = din("rg_lam", (DEPTH, 2, D_RNN))
    gla_wg2 = din("gla_wg2", (DEPTH, 2, 16, DK)); gla_bg = din("gla_bg", (DEPTH, 2, DK))
    gla_norm_w = din("gla_norm_w", (DEPTH, 256))
    w_proj_a = din("w_proj_a", (DEPTH, D_RNN, D)); w_proj_b = din("w_proj_b", (DEPTH, DV, D))
    w_out = din("w_out", (DEPTH, D, D))
    ln_mix_g = din("ln_mix_g", (DEPTH, D)); ln_mix_b = din("ln_mix_b", (DEPTH, D))
    w_up = din("w_up", (DEPTH, D, 2 * D_FF))
    conv_ff_w = din("conv_ff_w", (DEPTH, 3, D_FF)); conv_ff_b = din("conv_ff_b", (DEPTH, D_FF))
    w_down = din("w_down", (DEPTH, D_FF, D))
    ln_ffn_g = din("ln_ffn_g", (DEPTH, D)); ln_ffn_b = din("ln_ffn_b", (DEPTH, D))
    y_out = nc.dram_tensor("y", [T, D], F32, kind="ExternalOutput").ap()

    def scr(name, shape, dt):
        return nc.dram_tensor(name, list(shape), dt, kind="Internal").ap()

    wq_in = [scr("wq_in%d" % l, (128, 16, D_IN), BF16) for l in range(DEPTH)]
    wq_pa = [scr("wq_pa%d" % l, (128, 8, D), BF16) for l in range(DEPTH)]
    wq_pb = [scr("wq_pb%d" % l, (128, 8, D), BF16) for l in range(DEPTH)]
    wq_out = [scr("wq_out%d" % l, (128, 16, D), BF16) for l in range(DEPTH)]
    wq_up = [scr("wq_up%d" % l, (128, 16, 2 * D_FF), BF16) for l in range(DEPTH)]
    wq_dn = [scr("wq_dn%d" % l, (128, 48, D), BF16) for l in range(DEPTH)]
    xres = scr("xres", (D, T), F32)
    xbf = scr("xbf", (D, T + 4), BF16)
    x1res = scr("x1res", (D, T), F32)
    x1bf = scr("x1bf", (D, T + 4), BF16)
    A_s = [scr("A%d" % d, (D_RNN, T), F32) for d in range(2)]
    U_s = [scr("U%d" % d, (D_RNN, T), F32) for d in range(2)]
    GRG = scr("GRG", (D_RNN, T), F32)
    SOG = scr("SOG", (DV, T), F32)
    SMA = scr("SMA", (D, T), F32)
    SMB = scr("SMB", (D, T), F32)
    QD_s = [scr("QD%d" % d, (DK, T), BF16) for d in range(2)]
    KD_s = [scr("KD%d" % d, (DK, T), BF16) for d in range(2)]
    KE_s = [scr("KE%d" % d, (T, DK), BF16) for d in range(2)]
    V_s = scr("V", (T, DV), BF16)
    DEC_s = [scr("DEC%d" % d, (DK, NCH), F32) for d in range(2)]
    HG = scr("HG", (D_RNN, T), BF16)
    O1 = scr("O1", (DV, T), F32)
    O2 = scr("O2", (DV, T), F32)

    with ExitStack() as es:
        S = Sched(nc, es)
        PE, ACT, DVE, POOL, SP = "pe", "act", "dve", "pool", "sp"

        uniq = [0]

        def sb(stack, name, shape, dt):
            uniq[0] += 1
            return stack.enter_context(nc.sbuf_tensor("%s_%d" % (name, uniq[0]), list(shape), dt))

        Dm = sb(es, "Dm", (128, 128), F32)
        MASKF = sb(es, "MASKF", (128, 128), F32)
        MASKB = sb(es, "MASKB", (128, 128), F32)
        IDENT = sb(es, "IDENT", (128, 128), F32)
        TF = sb(es, "TF", (128, 128), F32)
        TB = sb(es, "TB", (128, 128), F32)
        TEF = sb(es, "TEF", (128, 128), F32)
        TEB = sb(es, "TEB", (128, 128), F32)
        ONESD_B = sb(es, "ONESD_B", (128, 128), BF16)
        ONESV_B = sb(es, "ONESV_B", (128, 128), BF16)
        ONEROW = sb(es, "ONEROW", (1, 128), F32)
        FLG = sb(es, "FLG", (128, 4), F32)
        bC = Buf("consts")
        S.op(POOL, lambda: nc.gpsimd.iota(Dm[:], pattern=[[1, 128]], base=0, channel_multiplier=-1,
                                         allow_small_or_imprecise_dtypes=True), writes=[bC])
        S.op(DVE, lambda: nc.vector.tensor_single_scalar(out=MASKF[:], in_=Dm[:], scalar=0.0, op=ALU.is_ge), reads=[bC], writes=[bC])
        S.op(DVE, lambda: nc.vector.tensor_single_scalar(out=MASKB[:], in_=Dm[:], scalar=0.0, op=ALU.is_le), reads=[bC], writes=[bC])
        S.op(DVE, lambda: nc.vector.tensor_single_scalar(out=IDENT[:], in_=Dm[:], scalar=0.0, op=ALU.is_equal), reads=[bC], writes=[bC])
        S.op(DVE, lambda: nc.vector.tensor_scalar(out=TF[:], in0=MASKF[:], scalar1=-1.0 / 16, scalar2=None, op0=ALU.mult), reads=[bC], writes=[bC])
        S.op(DVE, lambda: nc.vector.tensor_scalar(out=TB[:], in0=MASKB[:], scalar1=-1.0 / 16, scalar2=None, op0=ALU.mult), reads=[bC], writes=[bC])
        S.op(DVE, lambda: nc.vector.tensor_scalar(out=TEF[:], in0=MASKF[:], scalar1=1.0 / 16, scalar2=-1.0 / 16, op0=ALU.mult, op1=ALU.add), reads=[bC], writes=[bC])
        S.op(DVE, lambda: nc.vector.tensor_scalar(out=TEB[:], in0=MASKB[:], scalar1=1.0 / 16, scalar2=-1.0 / 16, op0=ALU.mult, op1=ALU.add), reads=[bC], writes=[bC])
        S.op(DVE, lambda: nc.vector.memset(ONEROW[:], 1.0), writes=[bC])
        S.op(DVE, lambda: nc.vector.memset(ONESD_B[:], 1.0 / D), writes=[bC])
        S.op(DVE, lambda: nc.vector.memset(ONESV_B[:], 1.0 / 256), writes=[bC])
        S.dma(SP, FLG[:], flags_in, bC, writes=[bC])

        bWa = [Buf("wcastA%d" % l) for l in range(DEPTH)]
        bWb = [Buf("wcastB%d" % l) for l in range(DEPTH)]

        def cast_jobs(l, part):
            lst = ((w_in[l], wq_in[l], 16),) if part == 0 else ((w_proj_a[l], wq_pa[l], 8), (w_proj_b[l], wq_pb[l], 8),
                                                                 (w_out[l], wq_out[l], 16), (w_up[l], wq_up[l], 16), (w_down[l], wq_dn[l], 48))
            buf = bWa[l] if part == 0 else bWb[l]
            jobs = []
            for (src_, dst, kcs) in lst:
                sv = src_.rearrange("(kc p) n -> p kc n", p=128)
                for kc in range(kcs):
                    jobs.append(lambda dst=dst, sv=sv, kc=kc, buf=buf: S.dma(POOL, dst[:, kc, :], sv[:, kc, :], buf, writes=[buf]))
            return jobs
        cast_queue = []

        def cast_some(n):
            for _ in range(n):
                if cast_queue:
                    cast_queue.pop(0)()
        for j_ in cast_jobs(0, 0):
            j_()

        banks = [es.enter_context(nc.psum_tensor("ps%d" % i, [128, 512], F32)) for i in range(8)]
        bbank = [Buf("bank%d" % i) for i in range(8)]
        bank_rr = [0]

        def psum():
            i = bank_rr[0]
            bank_rr[0] = (i + 1) % 8
            return banks[i], bbank[i]

        class Ring:
            def __init__(self, stack, name, shape, dt, n):
                self.t = [sb(stack, "%s%d" % (name, i), shape, dt) for i in range(n)]
                self.b = [Buf("%s%d" % (name, i)) for i in range(n)]
                self.i = 0

            def get(self):
                i = self.i
                self.i = (i + 1) % len(self.t)
                return self.t[i], self.b[i]

        def mm_group(ps, pb, out_ap, pairs, extra_reads=()):
            n = len(pairs)
            for j, (l, r, bufs) in enumerate(pairs):
                S.op(PE, lambda l=l, r=r, j=j: nc.tensor.matmul(out_ap, l, r, start=(j == 0), stop=(j == n - 1)),
                     reads=list(bufs) + list(extra_reads), writes=[pb], signal=(j == n - 1))

        def ln_fm(stack_bufs, r, rbs, gt, bt, out_cb, obf_ring=None, do_norm=True):
            sqr, st = stack_bufs
            psm, pbm = psum()
            psq, pbq = psum()
            for kc in range(16):
                rc, rcb = sqr.get()
                S.op(ACT, lambda kc=kc, rc=rc: nc.scalar.activation(out=rc[:], in_=r[:, kc, :], func=AF.Copy), reads=[rbs[kc]], writes=[rcb])
                S.op(PE, lambda kc=kc, rc=rc: nc.tensor.matmul(psm[:], ONESD_B[:], rc[:], start=(kc == 0), stop=(kc == 15)),
                     reads=[rcb, bC], writes=[pbm], signal=True)
                sq, sqb = sqr.get()
                S.op(ACT, lambda kc=kc, sq=sq: nc.scalar.activation(out=sq[:], in_=r[:, kc, :], func=AF.Square), reads=[rbs[kc]], writes=[sqb])
                S.op(PE, lambda kc=kc, sq=sq: nc.tensor.matmul(psq[:], ONESD_B[:], sq[:], start=(kc == 0), stop=(kc == 15)),
                     reads=[sqb, bC], writes=[pbq], signal=True)
            mean, msq, rstd, nmr = st
            stb = st_buf
            S.op(DVE, lambda: nc.vector.tensor_copy(mean[:], psm[:]), reads=[pbm], writes=[stb])
            S.op(DVE, lambda: nc.vector.tensor_tensor(out=msq[:], in0=mean[:], in1=mean[:], op=ALU.mult), reads=[stb], writes=[stb])
            S.op(DVE, lambda: nc.vector.tensor_tensor(out=msq[:], in0=psq[:], in1=msq[:], op=ALU.subtract), reads=[pbq, stb], writes=[stb])
            S.op(ACT, lambda: nc.scalar.activation(out=rstd[:], in_=msq[:], func=AF.Sqrt, bias=EPSC[:, 0:1]), reads=[stb, bC], writes=[stb])
            S.op(DVE, lambda: nc.vector.reciprocal(out=rstd[:], in_=rstd[:]), reads=[stb], writes=[stb])
            S.op(DVE, lambda: nc.vector.scalar_tensor_tensor(out=nmr[:], in0=mean[:], scalar=-1.0, in1=rstd[:], op0=ALU.mult, op1=ALU.mult),
                 reads=[stb], writes=[stb])
            if not do_norm:
                return
            for kc in range(16):
                ln_norm(kc, st, stb, r, rbs, gt, bt, out_cb, obf_ring)

        def ln_norm(kc, st, stb, r, rbs, gt, bt, out_cb, obf_ring=None):
            mean, msq, rstd, nmr = st
            if True:
                rb = rbs[kc]
                S.op(DVE, lambda kc=kc: nc.vector.tensor_tensor(out=r[:, kc, :], in0=r[:, kc, :], in1=rstd[:], op=ALU.mult), reads=[rb, stb], writes=[rb])
                if kc % 2 == 0:
                    S.op(POOL, lambda kc=kc: nc.gpsimd.tensor_tensor(out=r[:, kc, :], in0=r[:, kc, :], in1=nmr[:], op=ALU.add), reads=[rb, stb], writes=[rb])
                else:
                    S.op(DVE, lambda kc=kc: nc.vector.tensor_tensor(out=r[:, kc, :], in0=r[:, kc, :], in1=nmr[:], op=ALU.add), reads=[rb, stb], writes=[rb])
                ob = obb = None
                if obf_ring is not None:
                    ob, obb = obf_ring.get()
                    S.op(ACT, lambda kc=kc, ob=ob: nc.scalar.activation(out=ob[:], in_=r[:, kc, :], func=AF.Identity,
                                                                        scale=gt[:, kc:kc + 1], bias=bt[:, kc:kc + 1]), reads=[rb, bP], writes=[obb])
                S.op(ACT, lambda kc=kc: nc.scalar.activation(out=r[:, kc, :], in_=r[:, kc, :], func=AF.Identity,
                                                             scale=gt[:, kc:kc + 1], bias=bt[:, kc:kc + 1]), reads=[rb, bP], writes=[rb])
                out_cb(kc, ob, obb)

        EPSC = sb(es, "EPSC", (128, 1), F32)
        S.op(DVE, lambda: nc.vector.memset(EPSC[:], EPS), writes=[bC])
        bP = Buf("params")

        def load_cols(stack, name, src_1d, ncols):
            t = sb(stack, name, (128, ncols), F32)
            S.dma(SP, t[:], src_1d.rearrange("(c p) -> p c", p=128), bP, writes=[bP], allow_slow_non_contiguous=True)
            return t

        g_in_t = load_cols(es, "g_in_t", ln_in_g, 16)
        b_in_t = load_cols(es, "b_in_t", ln_in_b, 16)

        with ExitStack() as ps0:
            xin_r = Ring(ps0, "xin", (128, D), F32, 2)
            r0 = sb(ps0, "r0", (128, 16, NT), F32); r0bs = [Buf("r0_%d" % k) for k in range(16)]
            sqr = Ring(ps0, "sq0", (128, NT), BF16, 4)
            st = [sb(ps0, "st0_%d" % i, (128, NT), F32) for i in range(4)]
            st_buf = Buf("st0")
            obf_r = Ring(ps0, "obf0", (128, NT), BF16, 3)
            for i in range(NTILE):
                t0 = i * NT
                for c in range(4):
                    xt, xb = xin_r.get()
                    S.dma(SP, xt[:], x_in[t0 + c * 128:t0 + (c + 1) * 128, :], xb, writes=[xb])
                    for k4 in range(4):
                        ps, pb = psum()
                        for k in range(4):
                            kc = k4 * 4 + k
                            S.op(PE, lambda kc=kc, k=k, ps=ps, xt=xt: nc.tensor.transpose(ps[:, k * 128:(k + 1) * 128], xt[:, kc * 128:(kc + 1) * 128], IDENT[:]),
                                 reads=[xb, bC], writes=[pb], signal=(k == 3))
                        eng = ACT if (k4 % 2 == 0) else DVE
                        for k in range(4):
                            kc = k4 * 4 + k
                            if eng == ACT:
                                S.op(ACT, lambda kc=kc, k=k, ps=ps: nc.scalar.activation(out=r0[:, kc, c * 128:(c + 1) * 128], in_=ps[:, k * 128:(k + 1) * 128], func=AF.Copy),
                                     reads=[pb], writes=[r0bs[kc]])
                            else:
                                S.op(DVE, lambda kc=kc, k=k, ps=ps: nc.vector.tensor_copy(r0[:, kc, c * 128:(c + 1) * 128], ps[:, k * 128:(k + 1) * 128]),
                                     reads=[pb], writes=[r0bs[kc]])

                def out_cb(kc, ob, obb, t0=t0):
                    S.dma(POOL, xres[kc * 128:(kc + 1) * 128, t0:t0 + NT], r0[:, kc, :], r0bs[kc], reads=[r0bs[kc]])
                    S.dma(POOL, xbf[kc * 128:(kc + 1) * 128, 2 + t0:2 + t0 + NT], ob[:], obb, reads=[obb])
                ln_fm((sqr, st), r0, r0bs, g_in_t, b_in_t, out_cb, obf_r)
            S.protected.add(bWb[0])
            for j_ in cast_jobs(0, 1):
                j_()
            S.barrier()

        class WStream:
            def __init__(self, stack, name, nslots, ncol=512):
                self.ring = Ring(stack, name, (128, 16, ncol), BF16, nslots)

            def load(self, src, kc0, kcn, c0, cn):
                t, b = self.ring.get()
                S.dma(SP, t[:, 0:kcn, 0:cn], src[:, kc0:kc0 + kcn, c0:c0 + cn], b, writes=[b])
                return t, b

        class WSeq:
            def __init__(self, ring, specs, ahead):
                self.ring, self.specs, self.ahead = ring, specs, ahead
                self.loaded = {}
                self.nxt = 0

            def _issue(self, upto):
                while self.nxt <= min(upto, len(self.specs) - 1):
                    src_, kc0, kcn, c0, cn = self.specs[self.nxt]
                    t, b = self.ring.get()
                    S.dma(SP, t[:, 0:kcn, 0:cn], src_[:, kc0:kc0 + kcn, c0:c0 + cn], b, writes=[b])
                    self.loaded[self.nxt] = (t, b)
                    self.nxt += 1

            def prime(self, n):
                self._issue(n - 1)

            def get(self, k):
                self._issue(k + self.ahead)
                return self.loaded.pop(k)

        for l in range(DEPTH):
            last = (l == DEPTH - 1)
            if l > 0:
                S.protected.discard(bWa[l])
                S.protected.discard(bWb[l])
                S.barrier()
            with ExitStack() as pl:
                binfm = sb(pl, "binfm", (128, 73), F32)
                blocks = []

                def addblk(c0, n):
                    for j in range(n):
                        blocks.append(c0 + j * 128)
                addblk(O_RX, 8); addblk(O_RG, 8); addblk(O_Q, 4); addblk(O_K, 4); addblk(O_OG, 8); addblk(O_MA, 16); addblk(O_MB, 16)
                blk_idx = {c0: j for j, c0 in enumerate(blocks)}
                for (c0, n) in ((O_RX, 8), (O_RG, 8), (O_Q, 4), (O_K, 4), (O_OG, 8), (O_MA, 16), (O_MB, 16)):
                    j0 = blk_idx[c0]
                    S.dma(SP, binfm[:, j0:j0 + n], b_in[l, c0:c0 + n * 128].rearrange("(c p) -> p c", p=128), bP, writes=[bP],
                          allow_slow_non_contiguous=True)
                bgz = sb(pl, "bgz", (16, 2), F32)
                S.dma(SP, bgz[:, 0:1], b_in[l, O_GF:O_GF + 16].rearrange("(p o) -> p o", o=1), bP, writes=[bP])
                S.dma(SP, bgz[:, 1:2], b_in[l, O_GB:O_GB + 16].rearrange("(p o) -> p o", o=1), bP, writes=[bP])
                cw = sb(pl, "cw", (128, 4, 8), F32)
                for j in range(4):
                    S.dma(SP, cw[:, j, :], conv_rg_w[l, j].rearrange("(c p) -> p c", p=128), bP, writes=[bP], allow_slow_non_contiguous=True)
                cb = load_cols(pl, "cb", conv_rg_b[l], 8)
                bat = sb(pl, "bat", (128, 2, 8), F32); bxt = sb(pl, "bxt", (128, 2, 8), F32)
                st1 = sb(pl, "st1", (128, 2, 8), F32); st2 = sb(pl, "st2", (128, 2, 8), F32)
                for d in range(2):
                    S.dma(SP, bat[:, d, :], rg_ba[l, d].rearrange("(c p) -> p c", p=128), bP, writes=[bP], allow_slow_non_contiguous=True)
                    S.dma(SP, bxt[:, d, :], rg_bx[l, d].rearrange("(c p) -> p c", p=128), bP, writes=[bP], allow_slow_non_contiguous=True)
                    S.dma(SP, st1[:, d, :], rg_lam[l, d].rearrange("(c p) -> p c", p=128), bP, writes=[bP], allow_slow_non_contiguous=True)
                S.op(ACT, lambda: nc.scalar.activation(out=st1[:], in_=st1[:], func=AF.Exp, scale=-1.0), reads=[bP], writes=[bP])
                S.op(ACT, lambda: nc.scalar.activation(out=st1[:], in_=st1[:], func=AF.Ln, bias=1.0), reads=[bP], writes=[bP])
                S.op(DVE, lambda: nc.vector.tensor_scalar(out=st2[:], in0=st1[:], scalar1=-16.0, scalar2=None, op0=ALU.mult), reads=[bP], writes=[bP])
                S.op(DVE, lambda: nc.vector.tensor_scalar(out=st1[:], in0=st1[:], scalar1=-8.0, scalar2=None, op0=ALU.mult), reads=[bP], writes=[bP])
                nwt = load_cols(pl, "nwt", gla_norm_w[l], 2)
                gm_t = load_cols(pl, "gm_t", ln_mix_g[l], 16); bm_t = load_cols(pl, "bm_t", ln_mix_b[l], 16)
                gf_t = load_cols(pl, "gf_t", ln_ffn_g[l], 16); bf_t = load_cols(pl, "bf_t", ln_ffn_b[l], 16)
                cfw = sb(pl, "cfw", (128, 3, 48), F32)
                for j in range(3):
                    S.dma(SP, cfw[:, j, :], conv_ff_w[l, j].rearrange("(c p) -> p c", p=128), bP, writes=[bP], allow_slow_non_contiguous=True)
                cfb = load_cols(pl, "cfb", conv_ff_b[l], 48)

                with ExitStack() as p1:
                    xT_r = Ring(p1, "xT", (128, 16, 516), BF16, 2)
                    xa_r = Ring(p1, "xa", (128, NT), F32, 2)
                    ws = WStream(p1, "w1", 3)
                    f32r = Ring(p1, "f1", (128, NT), F32, 6)
                    zrx_r = Ring(p1, "zrx", (128, 516), F32, 2)
                    xab_r = Ring(p1, "xab", (128, NT), BF16, 2)
                    zq = sb(p1, "zq", (128, 4, NT), F32); zqb = Buf("zq")
                    zk = sb(p1, "zk", (128, 4, NT), F32); zkb = Buf("zk")
                    ktok = sb(p1, "ktok", (128, 4, 512), F32); ktb = Buf("ktok")
                    vtok = sb(p1, "vtok", (128, 4, 1024), BF16); vtb = Buf("vtok")
                    zg = sb(p1, "zg", (16, 2, NT), F32); zgb = Buf("zg")
                    lg = sb(p1, "lg", (128, 2, 4, 512), F32); lgb = Buf("lg")
                    qd = [sb(p1, "qd%d" % d, (128, 4, NT), BF16) for d in range(2)]; qdb = [Buf("qd%d" % d) for d in range(2)]
                    kd = [sb(p1, "kd%d" % d, (128, 4, NT), BF16) for d in range(2)]; kdb = [Buf("kd%d" % d) for d in range(2)]
                    ke = [sb(p1, "ke%d" % d, (128, 4, 512), BF16) for d in range(2)]; keb = [Buf("ke%d" % d) for d in range(2)]
                    dect = [sb(p1, "dect%d" % d, (128, 4, 4), F32) for d in range(2)]; decb = [Buf("dec%d" % d) for d in range(2)]
                    BD = sb(p1, "BD", (128, 4, 8, 128), BF16)
                    S.op(POOL, lambda: nc.gpsimd.memset(BD[:], 0.0), writes=[bP])
                    for d in range(2):
                        for gi, wsrc in enumerate((rg_wa, rg_wx)):
                            g = 2 * d + gi
                            v = wsrc[l, d].rearrange("(c two) i j -> two i c j", two=2)
                            for two in range(2):
                                S.dma(POOL, BD[two * 64:(two + 1) * 64, g, :, two * 64:(two + 1) * 64], v[two], bP, writes=[bP])
                    brow = sb(p1, "brow", (1, 1536), F32)
                    S.dma(SP, brow[:, 0:512], b_in[l, O_K:O_K + 512].rearrange("(o n) -> o n", o=1), bP, writes=[bP])
                    S.dma(SP, brow[:, 512:1536], b_in[l, O_V:O_V + 1024].rearrange("(o n) -> o n", o=1), bP, writes=[bP])
                    bgrow = sb(p1, "bgrow", (1, 2, 512), F32)
                    S.dma(SP, bgrow[:], gla_bg[l].rearrange("(o d) n -> o d n", o=1), bP, writes=[bP])
                    wg2t = sb(p1, "wg2t", (16, 2, 512), F32)
                    S.dma(SP, wg2t[:], gla_wg2[l].rearrange("d r n -> r d n"), bP, writes=[bP])
                    W = wq_in[l]

                    for i in range(NTILE):
                        t0 = i * NT
                        sl = i // TPS
                        first_in_slot = (i % TPS == 0)
                        last_in_slot = (i % TPS == TPS - 1)
                        xT, xTb = xT_r.get()
                        S.dma(SP, xT[:, :, 0:515], xbf.rearrange("(kc p) t -> p kc t", p=128)[:, :, t0:t0 + 515], xTb, writes=[xTb])

                        def fm_group(c0, nblk, evac, halo=False):
                            for _ in fm_group_gen(c0, nblk, evac, halo):
                                pass

                        def fm_group_gen(c0, nblk, evac, halo=False):
                            wt, wb = ws.load(W, 0, 16, c0, nblk * 128)
                            for m in range(nblk):
                                ps, pb = psum()
                                mm_group(ps, pb, ps[:], [(wt[:, kc, m * 128:(m + 1) * 128], xT[:, kc, 2:514], [wb, xTb]) for kc in range(16)])
                                psh = pbh = None
                                if halo:
                                    psh, pbh = psum()
                                    mm_group(psh, pbh, psh[:, 0:2], [(wt[:, kc, m * 128:(m + 1) * 128], xT[:, kc, 0:2], [wb, xTb]) for kc in range(16)])
                                    mm_group(psh, pbh, psh[:, 2:3], [(wt[:, kc, m * 128:(m + 1) * 128], xT[:, kc, 514:515], [wb, xTb]) for kc in range(16)])
                                evac(m, ps, pb, psh, pbh)
                                yield m

                        def do_g():
                            wt, wb = ws.load(W, 0, 16, O_GF, 32)
                            for d in range(2):
                                ps, pb = psum()
                                mm_group(ps, pb, ps[0:16, :], [(wt[:, kc, d * 16:(d + 1) * 16], xT[:, kc, 2:514], [wb, xTb]) for kc in range(16)])
                                S.op(ACT, lambda d=d, ps=ps: nc.scalar.activation(out=zg[:, d, :], in_=ps[0:16, :], func=AF.Identity, bias=bgz[:, d:d + 1]),
                                     reads=[pb, bP], writes=[zgb])
                        def do_logits():
                            for d in range(2):
                                for c in range(4):
                                    ps, pb = psum()
                                    mm_group(ps, pb, ps[:], [(zg[:, d, c * 128:(c + 1) * 128], wg2t[:, d, :], [zgb, bP]),
                                                             (ONEROW[:, :], bgrow[:, d, :], [bC, bP])])
                                    tmp, tb = f32r.get()
                                    S.op(ACT, lambda ps=ps, tmp=tmp: nc.scalar.activation(out=tmp[:], in_=ps[:], func=AF.Exp, scale=-1.0), reads=[pb], writes=[tb])
                                    S.op(ACT, lambda d=d, c=c, tmp=tmp: nc.scalar.activation(out=lg[:, d, c, :], in_=tmp[:], func=AF.Ln, bias=1.0), reads=[tb], writes=[lgb])

                        def ev_q(m, ps, pb, psh, pbh):
                            S.op(ACT, lambda: nc.scalar.activation(out=zq[:, m, :], in_=ps[:], func=AF.Identity, bias=binfm[:, blk_idx[O_Q] + m:blk_idx[O_Q] + m + 1]),
                                 reads=[pb, bP], writes=[zqb])

                        def ev_k(m, ps, pb, psh, pbh):
                            S.op(ACT, lambda: nc.scalar.activation(out=zk[:, m, :], in_=ps[:], func=AF.Identity, bias=binfm[:, blk_idx[O_K] + m:blk_idx[O_K] + m + 1]),
                                 reads=[pb, bP], writes=[zkb])
                        def do_cumsum():
                            for d in range(2):
                                TT = TF if d == 0 else TB
                                for h in range(4):
                                    ps, pb = psum()
                                    for c in range(4):
                                        S.op(PE, lambda c=c, ps=ps, h=h, d=d, TT=TT: nc.tensor.matmul(ps[:, c * 128:(c + 1) * 128], lg[:, d, c, h * 128:(h + 1) * 128], TT[:],
                                                                                                  start=True, stop=True),
                                             reads=[lgb, bC], writes=[pb], signal=(c == 3))
                                    e1, e1b = f32r.get()
                                    S.op(ACT, lambda ps=ps, e1=e1: nc.scalar.activation(out=e1[:], in_=ps[:], func=AF.Exp), reads=[pb], writes=[e1b])
                                    S.op(DVE, lambda e1=e1, h=h, d=d: nc.vector.scalar_tensor_tensor(out=qd[d][:, h, :], in0=zq[:, h, :], scalar=QSCALE, in1=e1[:],
                                                                                                  op0=ALU.mult, op1=ALU.mult), reads=[zqb, e1b], writes=[qdb[d]])
                                    e2, e2b = f32r.get()
                                    S.op(ACT, lambda ps=ps, e2=e2: nc.scalar.activation(out=e2[:], in_=ps[:], func=AF.Exp, scale=-1.0), reads=[pb], writes=[e2b])
                                    S.op(POOL, lambda e2=e2, h=h, d=d: nc.gpsimd.tensor_tensor(out=kd[d][:, h, :], in0=zk[:, h, :], in1=e2[:], op=ALU.mult),
                                         reads=[zkb, e2b], writes=[kdb[d]])
                                    col = 127 if d == 0 else 0
                                    S.op(ACT, lambda ps=ps, h=h, d=d, col=col: nc.scalar.activation(out=dect[d][:, h, :], in_=ps[:, col::128], func=AF.Exp),
                                         reads=[pb], writes=[decb[d]])
                                S.dma(POOL, QD_s[d].rearrange("(h p) t -> p h t", p=128)[:, :, t0:t0 + NT], qd[d][:], qdb[d], reads=[qdb[d]])
                                S.dma(POOL, KD_s[d].rearrange("(h p) t -> p h t", p=128)[:, :, t0:t0 + NT], kd[d][:], kdb[d], reads=[kdb[d]])
                                S.dma(POOL, DEC_s[d].rearrange("(h p) n -> p h n", p=128)[:, :, i * 4:(i + 1) * 4], dect[d][:], decb[d], reads=[decb[d]])

                        def do_ktv():
                            wt, wb = ws.load(W, 0, 16, O_K, 512)
                            for c in range(4):
                                ps, pb = psum()
                                mm_group(ps, pb, ps[:], [(xT[:, kc, 2 + c * 128:2 + (c + 1) * 128], wt[:, kc, :], [wb, xTb]) for kc in range(16)]
                                         + [(ONEROW[:, :], brow[:, 0:512], [bC, bP])])
                                S.op(DVE, lambda c=c, ps=ps: nc.vector.tensor_copy(ktok[:, c, :], ps[:]), reads=[pb], writes=[ktb])
                            for vh in range(2):
                                wt, wb = ws.load(W, 0, 16, O_V + vh * 512, 512)
                                for c in range(4):
                                    ps, pb = psum()
                                    mm_group(ps, pb, ps[:], [(xT[:, kc, 2 + c * 128:2 + (c + 1) * 128], wt[:, kc, :], [wb, xTb]) for kc in range(16)]
                                             + [(ONEROW[:, :], brow[:, 512 + vh * 512:1024 + vh * 512], [bC, bP])])
                                    S.op(ACT, lambda c=c, ps=ps, vh=vh: nc.scalar.activation(out=vtok[:, c, vh * 512:(vh + 1) * 512], in_=ps[:], func=AF.Copy),
                                         reads=[pb], writes=[vtb])
                            S.dma(POOL, V_s.rearrange("(c p) n -> p c n", p=128)[:, i * 4:(i + 1) * 4, :], vtok[:], vtb, reads=[vtb])
                        def do_kend():
                            for d in range(2):
                                TE = TEF if d == 0 else TEB
                                for c in range(4):
                                    ps, pb = psum()
                                    S.op(PE, lambda ps=ps, TE=TE, d=d, c=c: nc.tensor.matmul(ps[:], TE[:], lg[:, d, c, :], start=True, stop=True),
                                         reads=[lgb, bC], writes=[pb], signal=True)
                                    e1, e1b = f32r.get()
                                    S.op(ACT, lambda ps=ps, e1=e1: nc.scalar.activation(out=e1[:], in_=ps[:], func=AF.Exp), reads=[pb], writes=[e1b])
                                    S.op(DVE, lambda e1=e1, d=d, c=c: nc.vector.tensor_tensor(out=ke[d][:, c, :], in0=ktok[:, c, :], in1=e1[:], op=ALU.mult),
                                         reads=[ktb, e1b], writes=[keb[d]])
                                S.dma(POOL, KE_s[d].rearrange("(c p) n -> p c n", p=128)[:, i * 4:(i + 1) * 4, :], ke[d][:], keb[d], reads=[keb[d]])

                        def do_rx():
                            for c8_ in range(8):
                                def ev_rx(m, ps, pb, psh, pbh, c8=c8_):
                                    bcol = binfm[:, blk_idx[O_RX] + c8:blk_idx[O_RX] + c8 + 1]
                                    z, zb = zrx_r.get()
                                    S.op(DVE, lambda: nc.vector.tensor_scalar(out=z[:, 2:514], in0=ps[:], scalar1=bcol, scalar2=None, op0=ALU.add), reads=[pb, bP], writes=[zb])
                                    if i == 0:
                                        S.op(DVE, lambda: nc.vector.memset(z[:, 0:2], 0.0), writes=[zb])
                                    else:
                                        S.op(DVE, lambda: nc.vector.tensor_scalar(out=z[:, 0:2], in0=psh[:, 0:2], scalar1=bcol, scalar2=None, op0=ALU.add),
                                             reads=[pbh, bP], writes=[zb])
                                        if first_in_slot:
                                            S.op(DVE, lambda: nc.vector.tensor_scalar(out=z[:, 0:2], in0=z[:, 0:2], scalar1=FLG[:, sl:sl + 1], scalar2=None, op0=ALU.mult),
                                                 reads=[zb, bC], writes=[zb])
                                    if i == NTILE - 1:
                                        S.op(DVE, lambda: nc.vector.memset(z[:, 514:515], 0.0), writes=[zb])
                                    else:
                                        S.op(DVE, lambda: nc.vector.tensor_scalar(out=z[:, 514:515], in0=psh[:, 2:3], scalar1=bcol, scalar2=None, op0=ALU.add),
                                             reads=[pbh, bP], writes=[zb])
                                        if last_in_slot:
                                            S.op(DVE, lambda: nc.vector.tensor_scalar(out=z[:, 514:515], in0=z[:, 514:515], scalar1=FLG[:, sl + 1:sl + 2], scalar2=None,
                                                                                      op0=ALU.mult), reads=[zb, bC], writes=[zb])
                                    xa, xab = xa_r.get()
                                    S.op(DVE, lambda: nc.vector.tensor_scalar(out=xa[:], in0=z[:, 0:512], scalar1=cw[:, 0, c8:c8 + 1], scalar2=cb[:, c8:c8 + 1],
                                                                              op0=ALU.mult, op1=ALU.add), reads=[zb, bP], writes=[xab])
                                    for j in range(1, 4):
                                        S.op(DVE,
                                             lambda j=j: nc.vector.scalar_tensor_tensor(out=xa[:], in0=z[:, j:j + 512], scalar=cw[:, j, c8:c8 + 1],
                                                                                                               in1=xa[:], op0=ALU.mult, op1=ALU.add),
                                             reads=[zb, xab, bP], writes=[xab])
                                    xb16, xb16b = xab_r.get()
                                    S.op(POOL, lambda: nc.gpsimd.tensor_copy(xb16[:], xa[:]), reads=[xab], writes=[xb16b])
                                    def gates(c8=c8, xa=xa, xab=xab, xb16=xb16, xb16b=xb16b):
                                        tl = []
                                        for d in range(2):
                                            psr, pbr = psum()
                                            mm_group(psr, pbr, psr[:], [(BD[:, 2 * d, c8, :], xb16[:], [bP, xb16b])])
                                            psi, pbi = psum()
                                            mm_group(psi, pbi, psi[:], [(BD[:, 2 * d + 1, c8, :], xb16[:], [bP, xb16b])])
                                            tl.append((psr, pbr, psi, pbi) + f32r.get() + f32r.get() + f32r.get())
                                        for d in range(2):
                                            psr, pbr, psi, pbi, rt, rtb, it, itb, at_, atb = tl[d]
                                            S.op(ACT, lambda: nc.scalar.activation(out=rt[:], in_=psr[:], func=AF.Sigmoid, bias=bat[:, d, c8:c8 + 1]), reads=[pbr, bP], writes=[rtb])
                                            S.op(ACT, lambda: nc.scalar.activation(out=it[:], in_=psi[:], func=AF.Sigmoid, bias=bxt[:, d, c8:c8 + 1]), reads=[pbi, bP], writes=[itb])
                                            S.op(POOL, lambda: nc.gpsimd.tensor_tensor(out=it[:], in0=it[:], in1=xa[:], op=ALU.mult), reads=[itb, xab], writes=[itb])
                                        for d in range(2):
                                            psr, pbr, psi, pbi, rt, rtb, it, itb, at_, atb = tl[d]
                                            S.op(ACT, lambda: nc.scalar.activation(out=at_[:], in_=rt[:], func=AF.Exp, scale=st1[:, d, c8:c8 + 1]), reads=[rtb, bP], writes=[atb])
                                            S.dma(POOL, A_s[d][c8 * 128:(c8 + 1) * 128, t0:t0 + NT], at_[:], atb, reads=[atb])
                                            S.op(DVE, lambda: nc.vector.tensor_tensor(out=rt[:], in0=at_[:], in1=at_[:], op=ALU.mult), reads=[atb], writes=[rtb])
                                        for d in range(2):
                                            psr, pbr, psi, pbi, rt, rtb, it, itb, at_, atb = tl[d]
                                            S.op(ACT, lambda: nc.scalar.activation(out=rt[:], in_=rt[:], func=AF.Sqrt, scale=-1.0, bias=1.0), reads=[rtb], writes=[rtb])
                                            S.op(DVE, lambda: nc.vector.tensor_tensor(out=it[:], in0=it[:], in1=rt[:], op=ALU.mult), reads=[itb, rtb], writes=[itb])
                                            S.dma(POOL, U_s[d][c8 * 128:(c8 + 1) * 128, t0:t0 + NT], it[:], itb, reads=[itb])
                                    prev = list(pending)
                                    del pending[:]
                                    pending.append(gates)
                                    for f_ in prev:
                                        f_()
                                yield from fm_group_gen(O_RX + c8_ * 128, 1, ev_rx, halo=True)

                        def simple(c0, groups, func, dst):
                            for g in groups:
                                def ev(m, ps, pb, psh, pbh, g=g):
                                    bi = blk_idx[c0] + g * 4 + m
                                    o, ob = f32r.get()
                                    S.op(ACT, lambda: nc.scalar.activation(out=o[:], in_=ps[:], func=func, bias=binfm[:, bi:bi + 1]), reads=[pb, bP], writes=[ob])
                                    r0_ = (g * 4 + m) * 128
                                    S.dma(POOL, dst[r0_:r0_ + 128, t0:t0 + NT], o[:], ob, reads=[ob])
                                fm_group(c0 + g * 512, 4, ev)
                        pending = []
                        do_g()
                        simple(O_RG, [0, 1], AF.Gelu_apprx_tanh, GRG)
                        do_logits()
                        fm_group(O_Q, 4, ev_q)
                        fm_group(O_K, 4, ev_k)
                        do_ktv()
                        do_cumsum()
                        simple(O_OG, [0, 1], AF.Silu, SOG)
                        do_kend()
                        bulk = [(O_MA, g_, SMA) for g_ in range(4)] + [(O_MB, g_, SMB) for g_ in range(4)]
                        for k_, _m in enumerate(do_rx()):
                            c0_, g_, dst_ = bulk[k_]
                            simple(c0_, [g_], AF.Sigmoid, dst_)
                        for f_ in pending:
                            f_()
                        del pending[:]
                    S.barrier()

                with ExitStack() as p2:
                    AtD = [sb(p2, "At%d" % d, (128, T), F32) for d in range(2)]; AbD = [Buf("At%d" % d) for d in range(2)]
                    UtD = [sb(p2, "Ut%d" % d, (128, T), F32) for d in range(2)]; UbD = [Buf("Ut%d" % d) for d in range(2)]
                    Hf = sb(p2, "Hf", (128, T), F32); Hfb = Buf("Hf")
                    Hb = sb(p2, "Hb", (128, T), F32); Hbb = Buf("Hb")
                    Ho = sb(p2, "Ho", (128, T // 4), BF16); Hob = Buf("Ho")
                    Q4 = [slice(q4 * (T // 4), (q4 + 1) * (T // 4)) for q4 in range(4)]
                    for c8 in range(8):
                        rows = slice(c8 * 128, (c8 + 1) * 128)
                        for (dst, dbuf, srct) in ((AtD[0], AbD[0], A_s[0]), (AtD[1], AbD[1], A_s[1]), (UtD[1], UbD[1], U_s[1]), (UtD[0], UbD[0], U_s[0])):
                            for cs in Q4:
                                S.dma(SP, dst[:, cs], srct[rows, cs], dbuf, writes=[dbuf])
                        for d in range(2):
                            Hd, Hdb = (Hf, Hfb) if d == 0 else (Hb, Hbb)
                            At, Ab, Ut, Ub = AtD[d], AbD[d], UtD[d], UbD[d]
                            order = range(NS) if d == 0 else range(NS - 1, -1, -1)
                            for n, s in enumerate(order):
                                lo, hi = s * SL, (s + 1) * SL
                                if d == 0:
                                    if n > 0:
                                        S.op(DVE, lambda: nc.vector.tensor_scalar(out=At[:, lo:lo + 1], in0=At[:, lo:lo + 1], scalar1=FLG[:, s:s + 1], scalar2=None,
                                                                                  op0=ALU.mult), reads=[Ab, bC], writes=[Ab])
                                    init = Hd[:, lo - 1:lo] if n > 0 else 0.0
                                    S.op(DVE, lambda: nc.vector.tensor_tensor_scan(out=Hd[:, lo:hi], data0=At[:, lo:hi], data1=Ut[:, lo:hi],
                                                                                   initial=init, op0=ALU.mult, op1=ALU.add),
                                         reads=[Ab, Ub, Hdb], writes=[Hdb])
                                else:
                                    if n > 0:
                                        S.op(DVE, lambda: nc.vector.tensor_scalar(out=At[:, hi - 1:hi], in0=At[:, hi - 1:hi], scalar1=FLG[:, s + 1:s + 2], scalar2=None,
                                                                                  op0=ALU.mult), reads=[Ab, bC], writes=[Ab])
                                    init = Hd[:, hi:hi + 1] if n > 0 else 0.0
                                    S.op(DVE, lambda: nc.vector.tensor_tensor_scan(out=Hd[:, lo:hi][:, ::-1], data0=At[:, lo:hi][:, ::-1],
                                                                                   data1=Ut[:, lo:hi][:, ::-1],
                                                                                   initial=init, op0=ALU.mult, op1=ALU.add),
                                         reads=[Ab, Ub, Hdb], writes=[Hdb])
                            if d == 0:
                                for cs in Q4:
                                    S.dma(SP, UtD[0][:, cs], GRG[rows, cs], UbD[0], writes=[UbD[0]])
                        for q4 in range(4):
                            cs = Q4[q4]
                            S.op(POOL, lambda: nc.gpsimd.tensor_tensor(out=Hf[:, cs], in0=Hf[:, cs], in1=Hb[:, cs], op=ALU.add), reads=[Hfb, Hbb], writes=[Hfb])
                            S.op(DVE, lambda: nc.vector.tensor_tensor(out=Ho[:], in0=Hf[:, cs], in1=UtD[0][:, cs], op=ALU.mult), reads=[Hfb, UbD[0]], writes=[Hob])
                            S.dma(POOL, HG[rows, cs], Ho[:], Hob, reads=[Hob])
                    S.barrier()

                if l + 1 < DEPTH:
                    S.protected.add(bWa[l + 1])
                    S.protected.add(bWb[l + 1])
                    cast_queue.extend(cast_jobs(l + 1, 0) + cast_jobs(l + 1, 1))
                with ExitStack() as p2:
                    qd_r = Ring(p2, "gq", (128, 4, NT), BF16, 2)
                    kd_r = Ring(p2, "gk", (128, 4, NT), BF16, 2)
                    ke_r = Ring(p2, "ge", (128, 4, 512), BF16, 2)
                    v_r = Ring(p2, "gv", (128, 4, 1024), BF16, 2)
                    o_r = Ring(p2, "go", (128, 8, NT), F32, 2)
                    of_r = Ring(p2, "gof", (128, 8, NT), F32, 2)
                    att_r = Ring(p2, "gatt", (128, 512), BF16, 3)
                    dec_t = sb(p2, "gdec", (128, 4, NCH), F32); dec_b = Buf("gdec")
                    Sf = sb(p2, "Sf", (128, 4, 256), F32); Sfb = Buf("Sf")
                    Sb_ = sb(p2, "Sb", (128, 4, 256), BF16); Sbb = Buf("Sb")
                    MASKF4 = sb(p2, "MASKF4", (128, 512), F32)
                    MASKB4 = sb(p2, "MASKB4", (128, 512), F32)
                    bM4 = Buf("mask4")
                    for h in range(4):
                        S.op(DVE, lambda: nc.vector.tensor_copy(MASKF4[:, h * 128:(h + 1) * 128], MASKF[:]), reads=[bC], writes=[bM4])
                        S.op(DVE, lambda: nc.vector.tensor_copy(MASKB4[:, h * 128:(h + 1) * 128], MASKB[:]), reads=[bC], writes=[bM4])
                    for d in range(2):
                        MASK4 = MASKF4 if d == 0 else MASKB4
                        S.dma(SP, dec_t[:], DEC_s[d].rearrange("(h p) n -> p h n", p=128), dec_b, writes=[dec_b])
                        S.op(DVE, lambda: nc.vector.memset(Sf[:], 0.0), writes=[Sfb])
                        S.op(POOL, lambda: nc.gpsimd.memset(Sb_[:], 0.0), writes=[Sbb])
                        tiles = list(range(NTILE)) if d == 0 else list(range(NTILE - 1, -1, -1))
                        chunks = list(range(4)) if d == 0 else list(range(3, -1, -1))
                        steps = [(i, c) for i in tiles for c in chunks]
                        tdata = {}

                        def load_tile(i):
                            t0 = i * NT
                            qt, qb = qd_r.get(); kt, kb = kd_r.get(); et, eb = ke_r.get(); vt, vb = v_r.get()
                            S.dma(SP, qt[:], QD_s[d].rearrange("(h p) t -> p h t", p=128)[:, :, t0:t0 + NT], qb, writes=[qb])
                            S.dma(SP, kt[:], KD_s[d].rearrange("(h p) t -> p h t", p=128)[:, :, t0:t0 + NT], kb, writes=[kb])
                            S.dma(SP, et[:], KE_s[d].rearrange("(c p) n -> p c n", p=128)[:, i * 4:(i + 1) * 4, :], eb, writes=[eb])
                            S.dma(SP, vt[:], V_s.rearrange("(c p) n -> p c n", p=128)[:, i * 4:(i + 1) * 4, :], vb, writes=[vb])
                            ot, ob = o_r.get()
                            oft = ofb = None
                            if d == 1:
                                oft, ofb = of_r.get()
                                S.dma(SP, oft[:], O1.rearrange("(j p) t -> p j t", p=128)[:, :, t0:t0 + NT], ofb, writes=[ofb])
                            tdata[i] = (qt, qb, kt, kb, et, eb, vt, vb, ot, ob, oft, ofb)

                        stA = {}

                        def stage_A(n):
                            i, c = steps[n]
                            if i not in tdata:
                                load_tile(i)
                            qt, qb, kt, kb, et, eb, vt, vb = tdata[i][:8]
                            cs = slice(c * 128, (c + 1) * 128)
                            psa, pba = banks[n % 2], bbank[n % 2]
                            for h in range(4):
                                mm_group(psa, pba, psa[:, h * 128:(h + 1) * 128], [(kt[:, h, cs], qt[:, h, cs], [kb, qb])])
                            kb0 = 2 + 2 * (n % 2)
                            kvb = [(banks[kb0], bbank[kb0]), (banks[kb0 + 1], bbank[kb0 + 1])]
                            for h in range(4):
                                psk, pbk = kvb[h // 2]
                                mm_group(psk, pbk, psk[:, (h % 2) * 256:(h % 2 + 1) * 256],
                                         [(et[:, c, h * 128:(h + 1) * 128], vt[:, c, h * 256:(h + 1) * 256], [eb, vb])])
                            stA[n] = (psa, pba, kvb)

                        stage_A(0)
                        for n in range(len(steps)):
                            i, c = steps[n]
                            qt, qb, kt, kb, et, eb, vt, vb, ot, ob, oft, ofb = tdata[i]
                            psa, pba, kvb = stA.pop(n)
                            gch = i * 4 + c
                            cs = slice(c * 128, (c + 1) * 128)
                            tok = gch * 128
                            am, amb = att_r.get()
                            S.op(DVE, lambda: nc.vector.tensor_tensor(out=am[:], in0=psa[:], in1=MASK4[:], op=ALU.mult), reads=[pba, bM4], writes=[amb])
                            if n + 1 < len(steps):
                                stage_A(n + 1)
                            if d == 0 and tok % SL == 0 and tok > 0:
                                fl = FLG[:, tok // SL:tok // SL + 1]
                            elif d == 1 and (tok + 128) % SL == 0 and tok + 128 < T:
                                fl = FLG[:, (tok + 128) // SL:(tok + 128) // SL + 1]
                            else:
                                fl = None
                            if fl is not None:
                                S.op(DVE, lambda: nc.vector.tensor_scalar(out=Sf[:], in0=Sf[:], scalar1=fl, scalar2=None, op0=ALU.mult), reads=[Sfb, bC], writes=[Sfb])
                                S.op(ACT, lambda: nc.scalar.activation(out=Sb_[:], in_=Sf[:], func=AF.Copy), reads=[Sfb], writes=[Sbb])
                            pso = [(banks[6], bbank[6]), (banks[7], bbank[7])]
                            for j in range(2):
                                ps_, pb_ = pso[j]
                                for h in range(4):
                                    mm_group(ps_, pb_, ps_[:, h * 128:(h + 1) * 128],
                                             [(vt[:, c, h * 256 + j * 128:h * 256 + (j + 1) * 128], am[:, h * 128:(h + 1) * 128], [vb, amb]),
                                              (Sb_[:, h, j * 128:(j + 1) * 128], qt[:, h, cs], [Sbb, qb])])
                            for j in range(2):
                                ps_, pb_ = pso[j]
                                pv = ps_[:].rearrange("p (h c) -> p h c", h=4)
                                if d == 0:
                                    S.op(ACT, lambda: nc.scalar.activation(out=ot[:, j::2, cs], in_=pv, func=AF.Copy), reads=[pb_], writes=[ob])
                                else:
                                    S.op(DVE, lambda: nc.vector.tensor_tensor(out=ot[:, j::2, cs], in0=pv, in1=oft[:, j::2, cs], op=ALU.add),
                                         reads=[pb_, ofb], writes=[ob])
                            for h in range(4):
                                psk, pbk = kvb[h // 2]
                                S.op(DVE, lambda: nc.vector.scalar_tensor_tensor(out=Sf[:, h, :], in0=Sf[:, h, :], scalar=dec_t[:, h, gch:gch + 1],
                                                                                 in1=psk[:, (h % 2) * 256:(h % 2 + 1) * 256], op0=ALU.mult, op1=ALU.add),
                                     reads=[Sfb, dec_b, pbk], writes=[Sfb])
                            S.op(ACT, lambda: nc.scalar.activation(out=Sb_[:], in_=Sf[:], func=AF.Copy), reads=[Sfb], writes=[Sbb])
                            cast_some(1)
                            if c == chunks[-1]:
                                dst = O1 if d == 0 else O2
                                S.dma(POOL, dst.rearrange("(j p) t -> p j t", p=128)[:, :, i * NT:(i + 1) * NT], ot[:], ob, reads=[ob])
                                del tdata[i]
                        if d == 1:
                            cast_some(10 ** 6)
                        S.barrier()

                if l == 0:
                    S.protected.discard(bWb[0])
                    S.barrier()
                with ExitStack() as p3:
                    wp_ring = Ring(p3, "w3p", (128, 8, 512), BF16, 4)
                    wo_ring = Ring(p3, "w3o", (128, 16, 512), BF16, 2)
                    wseq_p, wseq_o = {}, {}

                    def seq_p(i):
                        if i not in wseq_p:
                            specs = []
                            for g in range(4):
                                specs.append((wq_pa[l], 0, 8, g * 512, 512))
                                specs.append((wq_pb[l], 0, 8, g * 512, 512))
                            wseq_p[i] = WSeq(wp_ring, specs, 2)
                        return wseq_p[i]

                    def seq_o(i):
                        if i not in wseq_o:
                            wseq_o[i] = WSeq(wo_ring, [(wq_out[l], 0, 16, g * 512, 512) for g in range(4)], 1)
                        return wseq_o[i]
                    hg_r = Ring(p3, "hg", (128, 8, NT), BF16, 2)
                    ogt2 = [sb(p3, "ogt%d" % k, (128, 8, NT), BF16) for k in range(2)]; ogb2 = [Buf("ogt%d" % k) for k in range(2)]
                    mg2 = [sb(p3, "mg%d" % k, (128, 16, NT), BF16) for k in range(2)]; mgb2 = [Buf("mg%d" % k) for k in range(2)]
                    r3 = sb(p3, "r3", (128, 16, NT), F32); r3bs = [Buf("r3_%d" % k) for k in range(16)]
                    pc_r = Ring(p3, "pc3", (128, NT), F32, 6)
                    o2_r = Ring(p3, "o23", (128, 2, NT), F32, 2)
                    f32r = Ring(p3, "f3", (128, NT), F32, 2)
                    sqr = Ring(p3, "sq3", (128, NT), BF16, 4)
                    st = [sb(p3, "st3_%d" % i, (128, NT), F32) for i in range(4)]
                    st_buf = Buf("st3")
                    obf_r = Ring(p3, "obf3", (128, NT), BF16, 2)
                    hgs = {}

                    def rms_head(i, h):
                        tsl = slice(i * NT, (i + 1) * NT)
                        ogt, ogb = ogt2[i % 2], ogb2[i % 2]
                        if h == 0:
                            hgt, hgb = hg_r.get()
                            hgs[i] = (hgt, hgb)
                            S.dma(SP, hgt[:], HG.rearrange("(c p) t -> p c t", p=128)[:, :, tsl], hgb, writes=[hgb])
                        if True:
                            o2, o2b = o2_r.get()
                            S.dma(SP, o2[:], O2.rearrange("(j p) t -> p j t", p=128)[:, 2 * h:2 * h + 2, tsl], o2b, writes=[o2b])
                            psn, pbn = psum()
                            for j in range(2):
                                sq, sqb = sqr.get()
                                S.op(ACT, lambda: nc.scalar.activation(out=sq[:], in_=o2[:, j, :], func=AF.Square), reads=[o2b], writes=[sqb])
                                S.op(PE, lambda: nc.tensor.matmul(psn[:], ONESV_B[:], sq[:], start=(j == 0), stop=(j == 1)), reads=[sqb, bC], writes=[pbn], signal=True)
                            rs, rsb = f32r.get()
                            S.op(ACT, lambda: nc.scalar.activation(out=rs[:], in_=psn[:], func=AF.Sqrt, bias=EPSC[:, 0:1]), reads=[pbn, bC], writes=[rsb])
                            S.op(DVE, lambda: nc.vector.reciprocal(out=rs[:], in_=rs[:]), reads=[rsb], writes=[rsb])
                            for j in range(2):
                                S.op(DVE, lambda: nc.vector.scalar_tensor_tensor(out=o2[:, j, :], in0=o2[:, j, :], scalar=nwt[:, j:j + 1], in1=rs[:],
                                                                                 op0=ALU.mult, op1=ALU.mult), reads=[o2b, rsb, bP], writes=[o2b])
                                pc, pcb = pc_r.get()
                                rr = (2 * h + j) * 128
                                S.dma(SP, pc[:], SOG[rr:rr + 128, tsl], pcb, writes=[pcb])
                                S.op(POOL, lambda: nc.gpsimd.tensor_tensor(out=ogt[:, 2 * h + j, :], in0=o2[:, j, :], in1=pc[:], op=ALU.mult),
                                     reads=[o2b, pcb], writes=[ogb])

                    def proj_stage(i, hook=None):
                        tsl = slice(i * NT, (i + 1) * NT)
                        ogt, ogb = ogt2[i % 2], ogb2[i % 2]
                        mg, mgb = mg2[i % 2], mgb2[i % 2]
                        hgt, hgb = hgs.pop(i)
                        for g in range(4):
                            wa, wab = seq_p(i).get(2 * g)
                            wb_, wbb = seq_p(i).get(2 * g + 1)
                            for m in range(4):
                                mb = g * 4 + m
                                psa, pba = psum()
                                mm_group(psa, pba, psa[:], [(wa[:, kc, m * 128:(m + 1) * 128], hgt[:, kc, :], [wab, hgb]) for kc in range(8)])
                                psb, pbb = psum()
                                mm_group(psb, pbb, psb[:], [(wb_[:, kc, m * 128:(m + 1) * 128], ogt[:, kc, :], [wbb, ogb]) for kc in range(8)])
                                pa, pab = pc_r.get()
                                S.dma(SP, pa[:], SMA[mb * 128:(mb + 1) * 128, tsl], pab, writes=[pab])
                                pb2, pb2b = pc_r.get()
                                S.dma(SP, pb2[:], SMB[mb * 128:(mb + 1) * 128, tsl], pb2b, writes=[pb2b])
                                S.op(DVE, lambda: nc.vector.tensor_tensor(out=pa[:], in0=psa[:], in1=pa[:], op=ALU.mult), reads=[pba, pab], writes=[pab])
                                S.op(DVE, lambda: nc.vector.tensor_tensor(out=pb2[:], in0=psb[:], in1=pb2[:], op=ALU.mult), reads=[pbb, pb2b], writes=[pb2b])
                                S.op(POOL, lambda: nc.gpsimd.tensor_tensor(out=mg[:, mb, :], in0=pa[:], in1=pb2[:], op=ALU.add),
                                     reads=[pab, pb2b], writes=[mgb])
                                if hook is not None:
                                    hook(mb)

                    def wout_group(i, g):
                        tsl = slice(i * NT, (i + 1) * NT)
                        mg, mgb = mg2[i % 2], mgb2[i % 2]
                        if True:
                            wo, wob = seq_o(i).get(g)
                            for m in range(4):
                                mb = g * 4 + m
                                ps, pb = psum()
                                mm_group(ps, pb, ps[:], [(wo[:, kc, m * 128:(m + 1) * 128], mg[:, kc, :], [wob, mgb]) for kc in range(16)])
                                pc, pcb = pc_r.get()
                                S.dma(SP, pc[:], xres[mb * 128:(mb + 1) * 128, tsl], pcb, writes=[pcb])
                                S.op(DVE, lambda: nc.vector.scalar_tensor_tensor(out=r3[:, mb, :], in0=pc[:], scalar=ALPHA, in1=ps[:], op0=ALU.mult, op1=ALU.add),
                                     reads=[pcb, pb], writes=[r3bs[mb]])

                    def ln_stage(i):
                        t0 = i * NT

                        def out_cb(kc, ob_, obb):
                            S.dma(POOL, x1res[kc * 128:(kc + 1) * 128, t0:t0 + NT], r3[:, kc, :], r3bs[kc], reads=[r3bs[kc]])
                            S.dma(POOL, x1bf[kc * 128:(kc + 1) * 128, 1 + t0:1 + t0 + NT], ob_[:], obb, reads=[obb])
                        ln_fm((sqr, st), r3, r3bs, gm_t, bm_t, out_cb, obf_r, do_norm=False)
                        return lambda kc: ln_norm(kc, st, st_buf, r3, r3bs, gm_t, bm_t, out_cb, obf_r)

                    seq_p(0).prime(2)
                    for h in range(4):
                        rms_head(0, h)
                    proj_stage(0)
                    seq_o(0).prime(1)
                    for i in range(NTILE):
                        for g in range(4):
                            if i + 1 < NTILE:
                                rms_head(i + 1, g)
                            wout_group(i, g)
                        if i + 1 < NTILE:
                            seq_p(i + 1).prime(2)
                        normf = ln_stage(i)
                        if i + 1 < NTILE:
                            proj_stage(i + 1, hook=normf)
                            seq_o(i + 1).prime(1)
                        else:
                            for kc in range(16):
                                normf(kc)
                    S.barrier()

                with ExitStack() as p4:
                    ws = WStream(p4, "w4", 4, 256)
                    xT = sb(p4, "xT4", (128, 16, 516), BF16); xTb = Buf("xT4")
                    hdn = sb(p4, "hdn", (128, 48, NT), BF16); hdb = Buf("hdn")
                    r4 = sb(p4, "r4", (128, 16, NT), F32); r4bs = [Buf("r4_%d" % k) for k in range(16)]
                    ug_r = Ring(p4, "ug", (128, 516), F32, 2)
                    f32r = Ring(p4, "f4", (128, NT), F32, 3)
                    pc_r = Ring(p4, "pc4", (128, NT), F32, 3)
                    sqr = Ring(p4, "sq4", (128, NT), BF16, 4)
                    st = [sb(p4, "st4_%d" % i, (128, NT), F32) for i in range(4)]
                    st_buf = Buf("st4")
                    obf_r = Ring(p4, "obf4", (128, NT), BF16, 2)
                    yst_r = Ring(p4, "yst", (128, 512), F32, 3)
                    Wu = wq_up[l]
                    pend4 = []
                    for i in range(NTILE):
                        t0 = i * NT
                        tsl = slice(t0, t0 + NT)
                        sl = i // TPS
                        first_in_slot = (i % TPS == 0)
                        last_in_slot = (i % TPS == TPS - 1)
                        S.dma(SP, xT[:, :, 0:514], x1bf.rearrange("(kc p) t -> p kc t", p=128)[:, :, t0:t0 + 514], xTb, writes=[xTb])
                        for g in range(24):
                            if pend4 and g < 16:
                                pend4[0](g)
                                if g == 15:
                                    pend4[1]()
                                    del pend4[:]
                            wg_, wgb = ws.load(Wu, 0, 16, g * 256, 256)
                            wv_, wvb = ws.load(Wu, 0, 16, D_FF + g * 256, 256)
                            for m in range(2):
                                hb_ = g * 2 + m
                                ps, pb = psum()
                                mm_group(ps, pb, ps[:], [(wg_[:, kc, m * 128:(m + 1) * 128], xT[:, kc, 1:513], [wgb, xTb]) for kc in range(16)])
                                psh, pbh = psum()
                                mm_group(psh, pbh, psh[:, 0:1], [(wg_[:, kc, m * 128:(m + 1) * 128], xT[:, kc, 0:1], [wgb, xTb]) for kc in range(16)])
                                mm_group(psh, pbh, psh[:, 1:2], [(wg_[:, kc, m * 128:(m + 1) * 128], xT[:, kc, 513:514], [wgb, xTb]) for kc in range(16)])
                                psv, pbv = psum()
                                mm_group(psv, pbv, psv[:], [(wv_[:, kc, m * 128:(m + 1) * 128], xT[:, kc, 1:513], [wvb, xTb]) for kc in range(16)])
                                u, ub = ug_r.get()
                                S.op(ACT, lambda u=u, ps=ps: nc.scalar.activation(out=u[:, 1:513], in_=ps[:], func=AF.Copy), reads=[pb], writes=[ub])
                                if i == 0:
                                    S.op(DVE, lambda u=u: nc.vector.memset(u[:, 0:1], 0.0), writes=[ub])
                                elif first_in_slot:
                                    S.op(DVE, lambda u=u, psh=psh: nc.vector.tensor_scalar(out=u[:, 0:1], in0=psh[:, 0:1], scalar1=FLG[:, sl:sl + 1], scalar2=None, op0=ALU.mult),
                                         reads=[pbh, bC], writes=[ub])
                                else:
                                    S.op(DVE, lambda u=u, psh=psh: nc.vector.tensor_copy(u[:, 0:1], psh[:, 0:1]), reads=[pbh], writes=[ub])
                                if i == NTILE - 1:
                                    S.op(DVE, lambda u=u: nc.vector.memset(u[:, 513:514], 0.0), writes=[ub])
                                elif last_in_slot:
                                    S.op(DVE, lambda u=u, psh=psh: nc.vector.tensor_scalar(out=u[:, 513:514], in0=psh[:, 1:2], scalar1=FLG[:, sl + 1:sl + 2], scalar2=None,
                                                                                          op0=ALU.mult), reads=[pbh, bC], writes=[ub])
                                else:
                                    S.op(DVE, lambda u=u, psh=psh: nc.vector.tensor_copy(u[:, 513:514], psh[:, 1:2]), reads=[pbh], writes=[ub])
                                acc, accb = f32r.get()
                                S.op(DVE, lambda u=u, acc=acc, hb_=hb_: nc.vector.tensor_scalar(out=acc[:], in0=u[:, 0:512], scalar1=cfw[:, 0, hb_:hb_ + 1], scalar2=cfb[:, hb_:hb_ + 1],
                                                                                             op0=ALU.mult, op1=ALU.add), reads=[ub, bP], writes=[accb])
                                S.op(DVE, lambda u=u, acc=acc, hb_=hb_: nc.vector.scalar_tensor_tensor(out=acc[:], in0=u[:, 1:513], scalar=cfw[:, 1, hb_:hb_ + 1], in1=acc[:],
                                                                                                     op0=ALU.mult, op1=ALU.add), reads=[ub, accb, bP], writes=[accb])
                                S.op(DVE, lambda u=u, acc=acc, hb_=hb_: nc.vector.scalar_tensor_tensor(out=acc[:], in0=u[:, 2:514], scalar=cfw[:, 2, hb_:hb_ + 1], in1=acc[:],
                                                                                                     op0=ALU.mult, op1=ALU.add), reads=[ub, accb, bP], writes=[accb])
                                S.op(ACT, lambda acc=acc: nc.scalar.activation(out=acc[:], in_=acc[:], func=AF.Gelu_apprx_tanh), reads=[accb], writes=[accb])
                                S.op(DVE, lambda acc=acc, psv=psv, hb_=hb_: nc.vector.tensor_tensor(out=hdn[:, hb_, :], in0=psv[:], in1=acc[:], op=ALU.mult),
                                     reads=[pbv, accb], writes=[hdb])
                        for g in range(8):
                            pss = [psum() for _ in range(2)]
                            for ks in range(3):
                                wd, wdb = ws.load(wq_dn[l], ks * 16, 16, g * 256, 256)
                                for m in range(2):
                                    ps, pb = pss[m]
                                    for kc in range(16):
                                        S.op(PE, lambda ps=ps, wd=wd, m=m, kc=kc, ks=ks: nc.tensor.matmul(ps[:], wd[:, kc, m * 128:(m + 1) * 128], hdn[:, ks * 16 + kc, :],
                                                                                                      start=(ks == 0 and kc == 0), stop=(ks == 2 and kc == 15)),
                                             reads=[wdb, hdb], writes=[pb], signal=(kc == 15))
                            for m in range(2):
                                mb = g * 2 + m
                                ps, pb = pss[m]
                                pc, pcb = pc_r.get()
                                S.dma(SP, pc[:], x1res[mb * 128:(mb + 1) * 128, tsl], pcb, writes=[pcb])
                                S.op(DVE, lambda pc=pc, ps=ps, mb=mb: nc.vector.scalar_tensor_tensor(out=r4[:, mb, :], in0=pc[:], scalar=ALPHA, in1=ps[:], op0=ALU.mult, op1=ALU.add),
                                     reads=[pcb, pb], writes=[r4bs[mb]])

                        if not last:
                            def out_cb(kc, ob_, obb, t0=t0):
                                S.dma(POOL, xres[kc * 128:(kc + 1) * 128, t0:t0 + NT], r4[:, kc, :], r4bs[kc], reads=[r4bs[kc]])
                                S.dma(POOL, xbf[kc * 128:(kc + 1) * 128, 2 + t0:2 + t0 + NT], ob_[:], obb, reads=[obb])
                            ln_fm((sqr, st), r4, r4bs, gf_t, bf_t, out_cb, obf_r, do_norm=False)
                            normf = (lambda kc, out_cb=out_cb: ln_norm(kc, st, st_buf, r4, r4bs, gf_t, bf_t, out_cb, obf_r))
                            finf = (lambda: None)
                        else:
                            ln_fm((sqr, st), r4, r4bs, gf_t, bf_t, None, None, do_norm=False)
                            normf = (lambda kc: ln_norm(kc, st, st_buf, r4, r4bs, gf_t, bf_t, lambda kc_, ob_, obb: None, None))

                            def finf(t0=t0):
                                for c in range(4):
                                    for k4 in range(4):
                                        ps, pb = psum()
                                        for k in range(4):
                                            kc = k4 * 4 + k
                                            S.op(PE, lambda: nc.tensor.transpose(ps[:, k * 128:(k + 1) * 128], r4[:, kc, c * 128:(c + 1) * 128], IDENT[:]),
                                                 reads=[r4bs[kc], bC], writes=[pb], signal=(k == 3))
                                        ys, ysb = yst_r.get()
                                        if k4 % 2 == 0:
                                            S.op(ACT, lambda: nc.scalar.activation(out=ys[:], in_=ps[:], func=AF.Copy), reads=[pb], writes=[ysb])
                                        else:
                                            S.op(DVE, lambda: nc.vector.tensor_copy(ys[:], ps[:]), reads=[pb], writes=[ysb])
                                        S.dma(POOL, y_out[t0 + c * 128:t0 + (c + 1) * 128, k4 * 512:(k4 + 1) * 512], ys[:], ysb, reads=[ysb])
                        if i + 1 < NTILE:
                            pend4[:] = [normf, finf]
                        else:
                            for kc in range(16):
                                normf(kc)
                            finf()
                    S.barrier()
        S.barrier()
    return nc


_W_NAMES = ["ln_in_g", "ln_in_b", "w_in", "b_in", "conv_rg_w", "conv_rg_b", "rg_wa", "rg_ba", "rg_wx", "rg_bx", "rg_lam",
            "gla_wg2", "gla_bg", "gla_norm_w", "w_proj_a", "w_proj_b", "w_out", "ln_mix_g", "ln_mix_b", "w_up",
            "conv_ff_w", "conv_ff_b", "w_down", "ln_ffn_g", "ln_ffn_b"]


def run_cores(core_x, core_flags, weights, NS, SL):
    nc = build_program(NS, SL)
    wd = {k: np.ascontiguousarray(np.asarray(weights[k], dtype=np.float32)) for k in _W_NAMES}
    in_maps = []
    for xc, fl in zip(core_x, core_flags):
        m = dict(wd)
        m["x"] = np.ascontiguousarray(xc, dtype=np.float32)
        m["flags"] = np.ascontiguousarray(np.broadcast_to(np.asarray(fl, dtype=np.float32)[None, :], (128, 4)))
        in_maps.append(m)
    res = run_bass_kernel_spmd(nc, in_maps, core_ids=list(range(len(core_x))))
    return [r["y"] for r in res.results]


def kernel(**inputs):
    xp = np.asarray(inputs["x_prompt"], dtype=np.float32)
    xs = np.asarray(inputs["x_sample"], dtype=np.float32)
    NS, SL = 4, 2048
    samp_of_core = [[0, 1, 2], [3, 4, 5], [6, 7, 8], [9, 10, 11], [12, 13], [14, 15]]
    core_x, core_flags = [], []
    for b in range(2):
        core_x.append(xp[b])
        core_flags.append([0.0, 1.0, 1.0, 1.0])
    for lst in samp_of_core:
        xc = np.zeros((NS * SL, D), np.float32)
        for j, s in enumerate(lst):
            xc[j * SL:(j + 1) * SL] = xs[s]
        core_x.append(xc)
        core_flags.append([0.0, 0.0, 0.0, 0.0])
    ys = run_cores(core_x, core_flags, inputs, NS, SL)
    y_prompt = np.stack([ys[0], ys[1]], axis=0).astype(np.float32)
    y_sample = np.zeros((16, SL, D), np.float32)
    for ci, lst in enumerate(samp_of_core):
        for j, s in enumerate(lst):
            y_sample[s] = ys[2 + ci][j * SL:(j + 1) * SL]
    return (y_prompt, y_sample)
```

```python
import numpy as np
from contextlib import ExitStack
import concourse.bass as bass
import concourse.mybir as mybir
from concourse.bass_utils import run_bass_kernel_spmd

F32 = mybir.dt.float32
BF16 = mybir.dt.bfloat16
AF = mybir.ActivationFunctionType
ALU = mybir.AluOpType

D = 2048
DEPTH = 2
D_RNN = 1024
DK = 512
DV = 1024
D_FF = 6144
D_IN = 9248
ALPHA = float((2.0 * DEPTH) ** 0.25)
EPS = 1e-5
QSCALE = float(128 ** -0.5)
NT = 512
O_RX, O_RG, O_Q, O_K, O_V, O_OG, O_GF, O_GB, O_MA, O_MB = 0, 1024, 2048, 2560, 3072, 4096, 5120, 5136, 5152, 7200


class Buf:
    __slots__ = ("name", "w", "r", "dsem", "dtot")

    def __init__(self, name):
        self.name = name
        self.w = None
        self.r = {}
        self.dsem = None
        self.dtot = 0


class Sched:
    def __init__(self, nc, es):
        self.nc = nc
        self.es = es
        self.eng = {"pe": nc.tensor, "act": nc.scalar, "dve": nc.vector, "pool": nc.gpsimd, "sp": nc.sync}
        self.sems = {}
        self.tot = {}
        self.waited = {k: {} for k in self.eng}
        for k in ("pe", "act", "dve", "pool"):
            self.sems[k] = es.enter_context(nc.semaphore("s_" + k))
            self.tot[k] = 0
        self.nd = 0
        self.free_dsems = []
        self.dbufs = []
        self.protected = set()

    def _dsem(self, buf):
        if buf.dsem is None:
            if self.free_dsems:
                key = self.free_dsems.pop()
            else:
                key = "d%d" % self.nd
                self.nd += 1
                self.sems[key] = self.es.enter_context(self.nc.semaphore("s_" + key))
                self.tot[key] = 0
            buf.dsem = key
            self.dbufs.append(buf)
        return buf.dsem

    def _wait(self, e, ev):
        if ev is None:
            return
        key, val = ev
        if self.waited[e].get(key, 0) >= val:
            return
        assert val <= self.tot[key], (key, val, self.tot[key])
        self.eng[e].wait_ge(self.sems[key], val)
        self.waited[e][key] = val

    def _deps(self, e, reads, writes, same_eng_key):
        for b in reads:
            if b.w is not None:
                if b.w[0] == same_eng_key and e == "pe":
                    continue
                self._wait(e, b.w)
        for b in writes:
            if b.w is not None and b.w[0] != same_eng_key:
                self._wait(e, b.w)
            for k_, v_ in b.r.items():
                if k_ != same_eng_key:
                    self._wait(e, (k_, v_))

    def op(self, e, fn, reads=(), writes=(), signal=True):
        self._deps(e, reads, writes, e)
        ins = fn()
        if signal:
            self.tot[e] += 1
            ins.then_inc(self.sems[e], 1)
            ev = (e, self.tot[e])
        else:
            ev = (e, self.tot[e] + 1)
        for b in reads:
            if b.r.get(ev[0], 0) < ev[1]:
                b.r[ev[0]] = ev[1]
        for b in writes:
            b.w = ev
            b.r = {}
        return ins

    def dma(self, q, out, in_, sb, reads=(), writes=(), **kw):
        key = self._dsem(sb)
        for b in writes:
            if b.w is not None and b.w[0] == key:
                b.w = None
        self._deps(q, reads, writes, None)
        ins = self.eng[q].dma_start(out=out, in_=in_, **kw)
        self.tot[key] += 16
        ins.then_inc(self.sems[key], 16)
        ev = (key, self.tot[key])
        for b in reads:
            b.r[key] = ev[1]
        for b in writes:
            b.w = ev
            b.r = {}
        return ins

    def barrier(self):
        prot = set(b.dsem for b in self.protected if b.dsem is not None)
        for e in self.eng:
            for key, v in self.tot.items():
                if v > 0 and key not in prot:
                    self._wait(e, (key, v))
        keep = []
        for b in self.dbufs:
            if b in self.protected:
                keep.append(b)
                continue
            b.dsem = None
            b.w = None
            b.r = {}
        self.dbufs = keep
        self.free_dsems = [k for k in self.sems if k.startswith("d") and k not in prot]


def build_program(NS, SL, final_only=True):
    T = NS * SL
    NTILE = T // NT
    TPS = SL // NT
    NCH = T // 128
    nc = bass.Bass("TRN2", target_bir_lowering=False)

    def din(name, shape):
        return nc.dram_tensor(name, list(shape), F32, kind="ExternalInput").ap()

    x_in = din("x", (T, D))
    flags_in = din("flags", (128, 4))
    ln_in_g = din("ln_in_g", (D,)); ln_in_b = din("ln_in_b", (D,))
    w_in = din("w_in", (DEPTH, D, D_IN)); b_in = din("b_in", (DEPTH, D_IN))
    conv_rg_w = din("conv_rg_w", (DEPTH, 4, D_RNN)); conv_rg_b = din("conv_rg_b", (DEPTH, D_RNN))
    rg_wa = din("rg_wa", (DEPTH, 2, 16, 64, 64)); rg_ba = din("rg_ba", (DEPTH, 2, D_RNN))
    rg_wx = din("rg_wx", (DEPTH, 2, 16, 64, 64)); rg_bx = din("rg_bx", (DEPTH, 2, D_RNN))
    rg_lam = din("rg_lam", (DEPTH, 2, D_RNN))
    gla_wg2 = din("gla_wg2", (DEPTH, 2, 16, DK)); gla_bg = din("gla_bg", (DEPTH, 2, DK))
    gla_norm_w = din("gla_norm_w", (DEPTH, 256))
    w_proj_a = din("w_proj_a", (DEPTH, D_RNN, D)); w_proj_b = din("w_proj_b", (DEPTH, DV, D))
    w_out = din("w_out", (DEPTH, D, D))
    ln_mix_g = din("ln_mix_g", (DEPTH, D)); ln_mix_b = din("ln_mix_b", (DEPTH, D))
    w_up = din("w_up", (DEPTH, D, 2 * D_FF))
    conv_ff_w = din("conv_ff_w", (DEPTH, 3, D_FF)); conv_ff_b = din("conv_ff_b", (DEPTH, D_FF))
    w_down = din("w_down", (DEPTH, D_FF, D))
    ln_ffn_g = din("ln_ffn_g", (DEPTH, D)); ln_ffn_b = din("ln_ffn_b", (DEPTH, D))
    y_out = nc.dram_tensor("y", [T, D], F32, kind="ExternalOutput").ap()

    def scr(name, shape, dt):
        return nc.dram_tensor(name, list(shape), dt, kind="Internal").ap()

    wq_in = [scr("wq_in%d" % l, (128, 16, D_IN), BF16) for l in range(DEPTH)]
    wq_pa = [scr("wq_pa%d" % l, (128, 8, D), BF16) for l in range(DEPTH)]
    wq_pb = [scr("wq_pb%d" % l, (128, 8, D), BF16) for l in range(DEPTH)]
    wq_out = [scr("wq_out%d" % l, (128, 16, D), BF16) for l in range(DEPTH)]
    wq_up = [scr("wq_up%d" % l, (128, 16, 2 * D_FF), BF16) for l in range(DEPTH)]
    wq_dn = [scr("wq_dn%d" % l, (128, 48, D), BF16) for l in range(DEPTH)]
    xres = scr("xres", (D, T), F32)
    xbf = scr("xbf", (D, T + 4), BF16)
    x1res = scr("x1res", (D, T), F32)
    x1bf = scr("x1bf", (D, T + 4), BF16)
    A_s = [scr("A%d" % d, (D_RNN, T), F32) for d in range(2)]
    U_s = [scr("U%d" % d, (D_RNN, T), F32) for d in range(2)]
    GRG = scr("GRG", (D_RNN, T), F32)
    SOG = scr("SOG", (DV, T), F32)
    SMA = scr("SMA", (D, T), F32)
    SMB = scr("SMB", (D, T), F32)
    QD_s = [scr("QD%d" % d, (DK, T), BF16) for d in range(2)]
    KD_s = [scr("KD%d" % d, (DK, T), BF16) for d in range(2)]
    KE_s = [scr("KE%d" % d, (T, DK), BF16) for d in range(2)]
    V_s = scr("V", (T, DV), BF16)
    DEC_s = [scr("DEC%d" % d, (DK, NCH), F32) for d in range(2)]
    HG = scr("HG", (D_RNN, T), BF16)
    O1 = scr("O1", (DV, T), F32)
    O2 = scr("O2", (DV, T), F32)

    with ExitStack() as es:
        S = Sched(nc, es)
        PE, ACT, DVE, POOL, SP = "pe", "act", "dve", "pool", "sp"

        uniq = [0]

        def sb(stack, name, shape, dt):
            uniq[0] += 1
            return stack.enter_context(nc.sbuf_tensor("%s_%d" % (name, uniq[0]), list(shape), dt))

        Dm = sb(es, "Dm", (128, 128), F32)
        MASKF = sb(es, "MASKF", (128, 128), F32)
        MASKB = sb(es, "MASKB", (128, 128), F32)
        IDENT = sb(es, "IDENT", (128, 128), F32)
        TF = sb(es, "TF", (128, 128), F32)
        TB = sb(es, "TB", (128, 128), F32)
        TEF = sb(es, "TEF", (128, 128), F32)
        TEB = sb(es, "TEB", (128, 128), F32)
        ONESD_B = sb(es, "ONESD_B", (128, 128), BF16)
        ONESV_B = sb(es, "ONESV_B", (128, 128), BF16)
        ONEROW = sb(es, "ONEROW", (1, 128), F32)
        FLG = sb(es, "FLG", (128, 4), F32)
        bC = Buf("consts")
        S.op(POOL, lambda: nc.gpsimd.iota(Dm[:], pattern=[[1, 128]], base=0, channel_multiplier=-1,
                                         allow_small_or_imprecise_dtypes=True), writes=[bC])
        S.op(DVE, lambda: nc.vector.tensor_single_scalar(out=MASKF[:], in_=Dm[:], scalar=0.0, op=ALU.is_ge), reads=[bC], writes=[bC])
        S.op(DVE, lambda: nc.vector.tensor_single_scalar(out=MASKB[:], in_=Dm[:], scalar=0.0, op=ALU.is_le), reads=[bC], writes=[bC])
        S.op(DVE, lambda: nc.vector.tensor_single_scalar(out=IDENT[:], in_=Dm[:], scalar=0.0, op=ALU.is_equal), reads=[bC], writes=[bC])
        S.op(DVE, lambda: nc.vector.tensor_scalar(out=TF[:], in0=MASKF[:], scalar1=-1.0 / 16, scalar2=None, op0=ALU.mult), reads=[bC], writes=[bC])
        S.op(DVE, lambda: nc.vector.tensor_scalar(out=TB[:], in0=MASKB[:], scalar1=-1.0 / 16, scalar2=None, op0=ALU.mult), reads=[bC], writes=[bC])
        S.op(DVE, lambda: nc.vector.tensor_scalar(out=TEF[:], in0=MASKF[:], scalar1=1.0 / 16, scalar2=-1.0 / 16, op0=ALU.mult, op1=ALU.add), reads=[bC], writes=[bC])
        S.op(DVE, lambda: nc.vector.tensor_scalar(out=TEB[:], in0=MASKB[:], scalar1=1.0 / 16, scalar2=-1.0 / 16, op0=ALU.mult, op1=ALU.add), reads=[bC], writes=[bC])
        S.op(DVE, lambda: nc.vector.memset(ONEROW[:], 1.0), writes=[bC])
        S.op(DVE, lambda: nc.vector.memset(ONESD_B[:], 1.0 / D), writes=[bC])
        S.op(DVE, lambda: nc.vector.memset(ONESV_B[:], 1.0 / 256), writes=[bC])
        S.dma(SP, FLG[:], flags_in, bC, writes=[bC])

        bWa = [Buf("wcastA%d" % l) for l in range(DEPTH)]
        bWb = [Buf("wcastB%d" % l) for l in range(DEPTH)]

        def cast_jobs(l, part):
            lst = ((w_in[l], wq_in[l], 16),) if part == 0 else ((w_proj_a[l], wq_pa[l], 8), (w_proj_b[l], wq_pb[l], 8),
                                                                 (w_out[l], wq_out[l], 16), (w_up[l], wq_up[l], 16), (w_down[l], wq_dn[l], 48))
            buf = bWa[l] if part == 0 else bWb[l]
            jobs = []
            for (src_, dst, kcs) in lst:
                sv = src_.rearrange("(kc p) n -> p kc n", p=128)
                for kc in range(kcs):
                    jobs.append(lambda dst=dst, sv=sv, kc=kc, buf=buf: S.dma(POOL, dst[:, kc, :], sv[:, kc, :], buf, writes=[buf]))
            return jobs
        cast_queue = []

        def cast_some(n):
            for _ in range(n):
                if cast_queue:
                    cast_queue.pop(0)()
        for j_ in cast_jobs(0, 0):
            j_()

        banks = [es.enter_context(nc.psum_tensor("ps%d" % i, [128, 512], F32)) for i in range(8)]
        bbank = [Buf("bank%d" % i) for i in range(8)]
        bank_rr = [0]

        def psum():
            i = bank_rr[0]
            bank_rr[0] = (i + 1) % 8
            return banks[i], bbank[i]

        class Ring:
            def __init__(self, stack, name, shape, dt, n):
                self.t = [sb(stack, "%s%d" % (name, i), shape, dt) for i in range(n)]
                self.b = [Buf("%s%d" % (name, i)) for i in range(n)]
                self.i = 0

            def get(self):
                i = self.i
                self.i = (i + 1) % len(self.t)
                return self.t[i], self.b[i]

        def mm_group(ps, pb, out_ap, pairs, extra_reads=()):
            n = len(pairs)
            for j, (l, r, bufs) in enumerate(pairs):
                S.op(PE, lambda l=l, r=r, j=j: nc.tensor.matmul(out_ap, l, r, start=(j == 0), stop=(j == n - 1)),
                     reads=list(bufs) + list(extra_reads), writes=[pb], signal=(j == n - 1))

        def ln_fm(stack_bufs, r, rbs, gt, bt, out_cb, obf_ring=None, do_norm=True):
            sqr, st = stack_bufs
            psm, pbm = psum()
            psq, pbq = psum()
            for kc in range(16):
                rc, rcb = sqr.get()
                S.op(ACT, lambda kc=kc, rc=rc: nc.scalar.activation(out=rc[:], in_=r[:, kc, :], func=AF.Copy), reads=[rbs[kc]], writes=[rcb])
                S.op(PE, lambda kc=kc, rc=rc: nc.tensor.matmul(psm[:], ONESD_B[:], rc[:], start=(kc == 0), stop=(kc == 15)),
                     reads=[rcb, bC], writes=[pbm], signal=True)
                sq, sqb = sqr.get()
                S.op(ACT, lambda kc=kc, sq=sq: nc.scalar.activation(out=sq[:], in_=r[:, kc, :], func=AF.Square), reads=[rbs[kc]], writes=[sqb])
                S.op(PE, lambda kc=kc, sq=sq: nc.tensor.matmul(psq[:], ONESD_B[:], sq[:], start=(kc == 0), stop=(kc == 15)),
                     reads=[sqb, bC], writes=[pbq], signal=True)
            mean, msq, rstd, nmr = st
            stb = st_buf
            S.op(DVE, lambda: nc.vector.tensor_copy(mean[:], psm[:]), reads=[pbm], writes=[stb])
            S.op(DVE, lambda: nc.vector.tensor_tensor(out=msq[:], in0=mean[:], in1=mean[:], op=ALU.mult), reads=[stb], writes=[stb])
            S.op(DVE, lambda: nc.vector.tensor_tensor(out=msq[:], in0=psq[:], in1=msq[:], op=ALU.subtract), reads=[pbq, stb], writes=[stb])
            S.op(ACT, lambda: nc.scalar.activation(out=rstd[:], in_=msq[:], func=AF.Sqrt, bias=EPSC[:, 0:1]), reads=[stb, bC], writes=[stb])
            S.op(DVE, lambda: nc.vector.reciprocal(out=rstd[:], in_=rstd[:]), reads=[stb], writes=[stb])
            S.op(DVE, lambda: nc.vector.scalar_tensor_tensor(out=nmr[:], in0=mean[:], scalar=-1.0, in1=rstd[:], op0=ALU.mult, op1=ALU.mult),
                 reads=[stb], writes=[stb])
            if not do_norm:
                return
            for kc in range(16):
                ln_norm(kc, st, stb, r, rbs, gt, bt, out_cb, obf_ring)

        def ln_norm(kc, st, stb, r, rbs, gt, bt, out_cb, obf_ring=None):
            mean, msq, rstd, nmr = st
            if True:
                rb = rbs[kc]
                S.op(DVE, lambda kc=kc: nc.vector.tensor_tensor(out=r[:, kc, :], in0=r[:, kc, :], in1=rstd[:], op=ALU.mult), reads=[rb, stb], writes=[rb])
                if kc % 2 == 0:
                    S.op(POOL, lambda kc=kc: nc.gpsimd.tensor_tensor(out=r[:, kc, :], in0=r[:, kc, :], in1=nmr[:], op=ALU.add), reads=[rb, stb], writes=[rb])
                else:
                    S.op(DVE, lambda kc=kc: nc.vector.tensor_tensor(out=r[:, kc, :], in0=r[:, kc, :], in1=nmr[:], op=ALU.add), reads=[rb, stb], writes=[rb])
                ob = obb = None
                if obf_ring is not None:
                    ob, obb = obf_ring.get()
                    S.op(ACT, lambda kc=kc, ob=ob: nc.scalar.activation(out=ob[:], in_=r[:, kc, :], func=AF.Identity,
                                                                        scale=gt[:, kc:kc + 1], bias=bt[:, kc:kc + 1]), reads=[rb, bP], writes=[obb])
                S.op(ACT, lambda kc=kc: nc.scalar.activation(out=r[:, kc, :], in_=r[:, kc, :], func=AF.Identity,
                                                             scale=gt[:, kc:kc + 1], bias=bt[:, kc:kc + 1]), reads=[rb, bP], writes=[rb])
                out_cb(kc, ob, obb)

        EPSC = sb(es, "EPSC", (128, 1), F32)
        S.op(DVE, lambda: nc.vector.memset(EPSC[:], EPS), writes=[bC])
        bP = Buf("params")

        def load_cols(stack, name, src_1d, ncols):
            t = sb(stack, name, (128, ncols), F32)
            S.dma(SP, t[:], src_1d.rearrange("(c p) -> p c", p=128), bP, writes=[bP], allow_slow_non_contiguous=True)
            return t

        g_in_t = load_cols(es, "g_in_t", ln_in_g, 16)
        b_in_t = load_cols(es, "b_in_t", ln_in_b, 16)

        with ExitStack() as ps0:
            xin_r = Ring(ps0, "xin", (128, D), F32, 2)
            r0s = [sb(ps0, "r0_%d" % q, (128, 16, NT), F32) for q in range(2)]
            r0bss = [[Buf("r0_%d_%d" % (q, k)) for k in range(16)] for q in range(2)]
            pend0 = []
            sqr = Ring(ps0, "sq0", (128, NT), BF16, 4)
            st = [sb(ps0, "st0_%d" % i, (128, NT), F32) for i in range(4)]
            st_buf = Buf("st0")
            obf_r = Ring(ps0, "obf0", (128, NT), BF16, 3)
            for i in range(NTILE):
                t0 = i * NT
                r0, r0bs = r0s[i % 2], r0bss[i % 2]
                for c in range(4):
                    if pend0:
                        for kc_ in range(4 * c, 4 * c + 4):
                            pend0[0](kc_)
                    xt, xb = xin_r.get()
                    S.dma(SP, xt[:], x_in[t0 + c * 128:t0 + (c + 1) * 128, :], xb, writes=[xb])
                    for k4 in range(4):
                        ps, pb = psum()
                        for k in range(4):
                            kc = k4 * 4 + k
                            S.op(PE, lambda kc=kc, k=k, ps=ps, xt=xt: nc.tensor.transpose(ps[:, k * 128:(k + 1) * 128], xt[:, kc * 128:(kc + 1) * 128], IDENT[:]),
                                 reads=[xb, bC], writes=[pb], signal=(k == 3))
                        eng = ACT if (k4 % 2 == 0) else DVE
                        for k in range(4):
                            kc = k4 * 4 + k
                            if eng == ACT:
                                S.op(ACT, lambda kc=kc, k=k, ps=ps: nc.scalar.activation(out=r0[:, kc, c * 128:(c + 1) * 128], in_=ps[:, k * 128:(k + 1) * 128], func=AF.Copy),
                                     reads=[pb], writes=[r0bs[kc]])
                            else:
                                S.op(DVE, lambda kc=kc, k=k, ps=ps: nc.vector.tensor_copy(r0[:, kc, c * 128:(c + 1) * 128], ps[:, k * 128:(k + 1) * 128]),
                                     reads=[pb], writes=[r0bs[kc]])

                def out_cb(kc, ob, obb, t0=t0, r0=r0, r0bs=r0bs):
                    S.dma(POOL, xres[kc * 128:(kc + 1) * 128, t0:t0 + NT], r0[:, kc, :], r0bs[kc], reads=[r0bs[kc]])
                    S.dma(POOL, xbf[kc * 128:(kc + 1) * 128, 2 + t0:2 + t0 + NT], ob[:], obb, reads=[obb])
                ln_fm((sqr, st), r0, r0bs, g_in_t, b_in_t, out_cb, obf_r, do_norm=False)
                normf0 = (lambda kc, r0=r0, r0bs=r0bs, out_cb=out_cb: ln_norm(kc, st, st_buf, r0, r0bs, g_in_t, b_in_t, out_cb, obf_r))
                if i + 1 < NTILE:
                    pend0[:] = [normf0]
                else:
                    del pend0[:]
                    for kc_ in range(16):
                        normf0(kc_)
            S.protected.add(bWb[0])
            for j_ in cast_jobs(0, 1):
                j_()
            S.barrier()

        class WStream:
            def __init__(self, stack, name, nslots, ncol=512):
                self.ring = Ring(stack, name, (128, 16, ncol), BF16, nslots)

            def load(self, src, kc0, kcn, c0, cn):
                t, b = self.ring.get()
                S.dma(SP, t[:, 0:kcn, 0:cn], src[:, kc0:kc0 + kcn, c0:c0 + cn], b, writes=[b])
                return t, b

        class WSeq:
            def __init__(self, ring, specs, ahead):
                self.ring, self.specs, self.ahead = ring, specs, ahead
                self.loaded = {}
                self.nxt = 0

            def _issue(self, upto):
                while self.nxt <= min(upto, len(self.specs) - 1):
                    src_, kc0, kcn, c0, cn = self.specs[self.nxt]
                    t, b = self.ring.get()
                    S.dma(SP, t[:, 0:kcn, 0:cn], src_[:, kc0:kc0 + kcn, c0:c0 + cn], b, writes=[b])
                    self.loaded[self.nxt] = (t, b)
                    self.nxt += 1

            def prime(self, n):
                self._issue(n - 1)

            def get(self, k):
                self._issue(k + self.ahead)
                return self.loaded.pop(k)

        for l in range(DEPTH):
            last = (l == DEPTH - 1)
            if l > 0:
                S.protected.discard(bWa[l])
                S.protected.discard(bWb[l])
                S.barrier()
            with ExitStack() as pl:
                binfm = sb(pl, "binfm", (128, 73), F32)
                blocks = []

                def addblk(c0, n):
                    for j in range(n):
                        blocks.append(c0 + j * 128)
                addblk(O_RX, 8); addblk(O_RG, 8); addblk(O_Q, 4); addblk(O_K, 4); addblk(O_OG, 8); addblk(O_MA, 16); addblk(O_MB, 16)
                blk_idx = {c0: j for j, c0 in enumerate(blocks)}
                for (c0, n) in ((O_RX, 8), (O_RG, 8), (O_Q, 4), (O_K, 4), (O_OG, 8), (O_MA, 16), (O_MB, 16)):
                    j0 = blk_idx[c0]
                    S.dma(SP, binfm[:, j0:j0 + n], b_in[l, c0:c0 + n * 128].rearrange("(c p) -> p c", p=128), bP, writes=[bP],
                          allow_slow_non_contiguous=True)
                bgz = sb(pl, "bgz", (16, 2), F32)
                S.dma(SP, bgz[:, 0:1], b_in[l, O_GF:O_GF + 16].rearrange("(p o) -> p o", o=1), bP, writes=[bP])
                S.dma(SP, bgz[:, 1:2], b_in[l, O_GB:O_GB + 16].rearrange("(p o) -> p o", o=1), bP, writes=[bP])
                cw = sb(pl, "cw", (128, 4, 8), F32)
                for j in range(4):
                    S.dma(SP, cw[:, j, :], conv_rg_w[l, j].rearrange("(c p) -> p c", p=128), bP, writes=[bP], allow_slow_non_contiguous=True)
                cb = load_cols(pl, "cb", conv_rg_b[l], 8)
                bat = sb(pl, "bat", (128, 2, 8), F32); bxt = sb(pl, "bxt", (128, 2, 8), F32)
                st1 = sb(pl, "st1", (128, 2, 8), F32); st2 = sb(pl, "st2", (128, 2, 8), F32)
                for d in range(2):
                    S.dma(SP, bat[:, d, :], rg_ba[l, d].rearrange("(c p) -> p c", p=128), bP, writes=[bP], allow_slow_non_contiguous=True)
                    S.dma(SP, bxt[:, d, :], rg_bx[l, d].rearrange("(c p) -> p c", p=128), bP, writes=[bP], allow_slow_non_contiguous=True)
                    S.dma(SP, st1[:, d, :], rg_lam[l, d].rearrange("(c p) -> p c", p=128), bP, writes=[bP], allow_slow_non_contiguous=True)
                S.op(ACT, lambda: nc.scalar.activation(out=st1[:], in_=st1[:], func=AF.Exp, scale=-1.0), reads=[bP], writes=[bP])
                S.op(ACT, lambda: nc.scalar.activation(out=st1[:], in_=st1[:], func=AF.Ln, bias=1.0), reads=[bP], writes=[bP])
                S.op(DVE, lambda: nc.vector.tensor_scalar(out=st2[:], in0=st1[:], scalar1=-16.0, scalar2=None, op0=ALU.mult), reads=[bP], writes=[bP])
                S.op(DVE, lambda: nc.vector.tensor_scalar(out=st1[:], in0=st1[:], scalar1=-8.0, scalar2=None, op0=ALU.mult), reads=[bP], writes=[bP])
                nwt = load_cols(pl, "nwt", gla_norm_w[l], 2)
                gm_t = load_cols(pl, "gm_t", ln_mix_g[l], 16); bm_t = load_cols(pl, "bm_t", ln_mix_b[l], 16)
                gf_t = load_cols(pl, "gf_t", ln_ffn_g[l], 16); bf_t = load_cols(pl, "bf_t", ln_ffn_b[l], 16)
                cfw = sb(pl, "cfw", (128, 3, 48), F32)
                for j in range(3):
                    S.dma(SP, cfw[:, j, :], conv_ff_w[l, j].rearrange("(c p) -> p c", p=128), bP, writes=[bP], allow_slow_non_contiguous=True)
                cfb = load_cols(pl, "cfb", conv_ff_b[l], 48)

                with ExitStack() as p1:
                    xT_r = Ring(p1, "xT", (128, 16, 516), BF16, 2)
                    xa_r = Ring(p1, "xa", (128, NT), F32, 2)
                    ws = WStream(p1, "w1", 3)
                    f32r = Ring(p1, "f1", (128, NT), F32, 6)
                    zrx_r = Ring(p1, "zrx", (128, 516), F32, 2)
                    xab_r = Ring(p1, "xab", (128, NT), BF16, 2)
                    zq = sb(p1, "zq", (128, 4, NT), F32); zqb = Buf("zq")
                    zk = sb(p1, "zk", (128, 4, NT), F32); zkb = Buf("zk")
                    ktok = sb(p1, "ktok", (128, 4, 512), F32); ktb = Buf("ktok")
                    vtok = sb(p1, "vtok", (128, 4, 1024), BF16); vtb = Buf("vtok")
                    zg = sb(p1, "zg", (16, 2, NT), F32); zgb = Buf("zg")
                    lg = sb(p1, "lg", (128, 2, 4, 512), F32); lgb = Buf("lg")
                    qd = [sb(p1, "qd%d" % d, (128, 4, NT), BF16) for d in range(2)]; qdb = [Buf("qd%d" % d) for d in range(2)]
                    kd = [sb(p1, "kd%d" % d, (128, 4, NT), BF16) for d in range(2)]; kdb = [Buf("kd%d" % d) for d in range(2)]
                    ke = [sb(p1, "ke%d" % d, (128, 4, 512), BF16) for d in range(2)]; keb = [Buf("ke%d" % d) for d in range(2)]
                    dect = [sb(p1, "dect%d" % d, (128, 4, 4), F32) for d in range(2)]; decb = [Buf("dec%d" % d) for d in range(2)]
                    BD = sb(p1, "BD", (128, 4, 8, 128), BF16)
                    S.op(POOL, lambda: nc.gpsimd.memset(BD[:], 0.0), writes=[bP])
                    for d in range(2):
                        for gi, wsrc in enumerate((rg_wa, rg_wx)):
                            g = 2 * d + gi
                            v = wsrc[l, d].rearrange("(c two) i j -> two i c j", two=2)
                            for two in range(2):
                                S.dma(POOL, BD[two * 64:(two + 1) * 64, g, :, two * 64:(two + 1) * 64], v[two], bP, writes=[bP])
                    brow = sb(p1, "brow", (1, 1536), F32)
                    S.dma(SP, brow[:, 0:512], b_in[l, O_K:O_K + 512].rearrange("(o n) -> o n", o=1), bP, writes=[bP])
                    S.dma(SP, brow[:, 512:1536], b_in[l, O_V:O_V + 1024].rearrange("(o n) -> o n", o=1), bP, writes=[bP])
                    bgrow = sb(p1, "bgrow", (1, 2, 512), F32)
                    S.dma(SP, bgrow[:], gla_bg[l].rearrange("(o d) n -> o d n", o=1), bP, writes=[bP])
                    wg2t = sb(p1, "wg2t", (16, 2, 512), F32)
                    S.dma(SP, wg2t[:], gla_wg2[l].rearrange("d r n -> r d n"), bP, writes=[bP])
                    W = wq_in[l]

                    for i in range(NTILE):
                        t0 = i * NT
                        sl = i // TPS
                        first_in_slot = (i % TPS == 0)
                        last_in_slot = (i % TPS == TPS - 1)
                        xT, xTb = xT_r.get()
                        S.dma(SP, xT[:, :, 0:515], xbf.rearrange("(kc p) t -> p kc t", p=128)[:, :, t0:t0 + 515], xTb, writes=[xTb])

                        def fm_group(c0, nblk, evac, halo=False):
                            for _ in fm_group_gen(c0, nblk, evac, halo):
                                pass

                        def fm_group_gen(c0, nblk, evac, halo=False):
                            wt, wb = ws.load(W, 0, 16, c0, nblk * 128)
                            for m in range(nblk):
                                ps, pb = psum()
                                mm_group(ps, pb, ps[:], [(wt[:, kc, m * 128:(m + 1) * 128], xT[:, kc, 2:514], [wb, xTb]) for kc in range(16)])
                                psh = pbh = None
                                if halo:
                                    psh, pbh = psum()
                                    mm_group(psh, pbh, psh[:, 0:2], [(wt[:, kc, m * 128:(m + 1) * 128], xT[:, kc, 0:2], [wb, xTb]) for kc in range(16)])
                                    mm_group(psh, pbh, psh[:, 2:3], [(wt[:, kc, m * 128:(m + 1) * 128], xT[:, kc, 514:515], [wb, xTb]) for kc in range(16)])
                                evac(m, ps, pb, psh, pbh)
                                yield m

                        def do_g():
                            wt, wb = ws.load(W, 0, 16, O_GF, 32)
                            for d in range(2):
                                ps, pb = psum()
                                mm_group(ps, pb, ps[0:16, :], [(wt[:, kc, d * 16:(d + 1) * 16], xT[:, kc, 2:514], [wb, xTb]) for kc in range(16)])
                                S.op(ACT, lambda d=d, ps=ps: nc.scalar.activation(out=zg[:, d, :], in_=ps[0:16, :], func=AF.Identity, bias=bgz[:, d:d + 1]),
                                     reads=[pb, bP], writes=[zgb])
                        def do_logits():
                            for d in range(2):
                                for c in range(4):
                                    ps, pb = psum()
                                    mm_group(ps, pb, ps[:], [(zg[:, d, c * 128:(c + 1) * 128], wg2t[:, d, :], [zgb, bP]),
                                                             (ONEROW[:, :], bgrow[:, d, :], [bC, bP])])
                                    tmp, tb = f32r.get()
                                    S.op(ACT, lambda ps=ps, tmp=tmp: nc.scalar.activation(out=tmp[:], in_=ps[:], func=AF.Exp, scale=-1.0), reads=[pb], writes=[tb])
                                    S.op(ACT, lambda d=d, c=c, tmp=tmp: nc.scalar.activation(out=lg[:, d, c, :], in_=tmp[:], func=AF.Ln, bias=1.0), reads=[tb], writes=[lgb])

                        def ev_q(m, ps, pb, psh, pbh):
                            S.op(ACT, lambda: nc.scalar.activation(out=zq[:, m, :], in_=ps[:], func=AF.Identity, bias=binfm[:, blk_idx[O_Q] + m:blk_idx[O_Q] + m + 1]),
                                 reads=[pb, bP], writes=[zqb])

                        def ev_k(m, ps, pb, psh, pbh):
                            S.op(ACT, lambda: nc.scalar.activation(out=zk[:, m, :], in_=ps[:], func=AF.Identity, bias=binfm[:, blk_idx[O_K] + m:blk_idx[O_K] + m + 1]),
                                 reads=[pb, bP], writes=[zkb])
                        def do_cumsum():
                            for d in range(2):
                                TT = TF if d == 0 else TB
                                for h in range(4):
                                    ps, pb = psum()
                                    for c in range(4):
                                        S.op(PE, lambda c=c, ps=ps, h=h, d=d, TT=TT: nc.tensor.matmul(ps[:, c * 128:(c + 1) * 128], lg[:, d, c, h * 128:(h + 1) * 128], TT[:],
                                                                                                  start=True, stop=True),
                                             reads=[lgb, bC], writes=[pb], signal=(c == 3))
                                    e1, e1b = f32r.get()
                                    S.op(ACT, lambda ps=ps, e1=e1: nc.scalar.activation(out=e1[:], in_=ps[:], func=AF.Exp), reads=[pb], writes=[e1b])
                                    S.op(DVE, lambda e1=e1, h=h, d=d: nc.vector.scalar_tensor_tensor(out=qd[d][:, h, :], in0=zq[:, h, :], scalar=QSCALE, in1=e1[:],
                                                                                                  op0=ALU.mult, op1=ALU.mult), reads=[zqb, e1b], writes=[qdb[d]])
                                    e2, e2b = f32r.get()
                                    S.op(ACT, lambda ps=ps, e2=e2: nc.scalar.activation(out=e2[:], in_=ps[:], func=AF.Exp, scale=-1.0), reads=[pb], writes=[e2b])
                                    S.op(POOL, lambda e2=e2, h=h, d=d: nc.gpsimd.tensor_tensor(out=kd[d][:, h, :], in0=zk[:, h, :], in1=e2[:], op=ALU.mult),
                                         reads=[zkb, e2b], writes=[kdb[d]])
                                    col = 127 if d == 0 else 0
                                    S.op(ACT, lambda ps=ps, h=h, d=d, col=col: nc.scalar.activation(out=dect[d][:, h, :], in_=ps[:, col::128], func=AF.Exp),
                                         reads=[pb], writes=[decb[d]])
                                S.dma(POOL, QD_s[d].rearrange("(h p) t -> p h t", p=128)[:, :, t0:t0 + NT], qd[d][:], qdb[d], reads=[qdb[d]])
                                S.dma(POOL, KD_s[d].rearrange("(h p) t -> p h t", p=128)[:, :, t0:t0 + NT], kd[d][:], kdb[d], reads=[kdb[d]])
                                S.dma(POOL, DEC_s[d].rearrange("(h p) n -> p h n", p=128)[:, :, i * 4:(i + 1) * 4], dect[d][:], decb[d], reads=[decb[d]])

                        def do_ktv():
                            wt, wb = ws.load(W, 0, 16, O_K, 512)
                            for c in range(4):
                                ps, pb = psum()
                                mm_group(ps, pb, ps[:], [(xT[:, kc, 2 + c * 128:2 + (c + 1) * 128], wt[:, kc, :], [wb, xTb]) for kc in range(16)]
                                         + [(ONEROW[:, :], brow[:, 0:512], [bC, bP])])
                                S.op(DVE, lambda c=c, ps=ps: nc.vector.tensor_copy(ktok[:, c, :], ps[:]), reads=[pb], writes=[ktb])
                            for vh in range(2):
                                wt, wb = ws.load(W, 0, 16, O_V + vh * 512, 512)
                                for c in range(4):
                                    ps, pb = psum()
                                    mm_group(ps, pb, ps[:], [(xT[:, kc, 2 + c * 128:2 + (c + 1) * 128], wt[:, kc, :], [wb, xTb]) for kc in range(16)]
                                             + [(ONEROW[:, :], brow[:, 512 + vh * 512:1024 + vh * 512], [bC, bP])])
                                    S.op(ACT, lambda c=c, ps=ps, vh=vh: nc.scalar.activation(out=vtok[:, c, vh * 512:(vh + 1) * 512], in_=ps[:], func=AF.Copy),
                                         reads=[pb], writes=[vtb])
                            S.dma(POOL, V_s.rearrange("(c p) n -> p c n", p=128)[:, i * 4:(i + 1) * 4, :], vtok[:], vtb, reads=[vtb])
                        def do_kend():
                            for d in range(2):
                                TE = TEF if d == 0 else TEB
                                for c in range(4):
                                    ps, pb = psum()
                                    S.op(PE, lambda ps=ps, TE=TE, d=d, c=c: nc.tensor.matmul(ps[:], TE[:], lg[:, d, c, :], start=True, stop=True),
                                         reads=[lgb, bC], writes=[pb], signal=True)
                                    e1, e1b = f32r.get()
                                    S.op(ACT, lambda ps=ps, e1=e1: nc.scalar.activation(out=e1[:], in_=ps[:], func=AF.Exp), reads=[pb], writes=[e1b])
                                    S.op(DVE, lambda e1=e1, d=d, c=c: nc.vector.tensor_tensor(out=ke[d][:, c, :], in0=ktok[:, c, :], in1=e1[:], op=ALU.mult),
                                         reads=[ktb, e1b], writes=[keb[d]])
                                S.dma(POOL, KE_s[d].rearrange("(c p) n -> p c n", p=128)[:, i * 4:(i + 1) * 4, :], ke[d][:], keb[d], reads=[keb[d]])

                        def do_rx():
                            for c8_ in range(8):
                                def ev_rx(m, ps, pb, psh, pbh, c8=c8_):
                                    bcol = binfm[:, blk_idx[O_RX] + c8:blk_idx[O_RX] + c8 + 1]
                                    z, zb = zrx_r.get()
                                    S.op(DVE, lambda: nc.vector.tensor_scalar(out=z[:, 2:514], in0=ps[:], scalar1=bcol, scalar2=None, op0=ALU.add), reads=[pb, bP], writes=[zb])
                                    if i == 0:
                                        S.op(DVE, lambda: nc.vector.memset(z[:, 0:2], 0.0), writes=[zb])
                                    else:
                                        S.op(DVE, lambda: nc.vector.tensor_scalar(out=z[:, 0:2], in0=psh[:, 0:2], scalar1=bcol, scalar2=None, op0=ALU.add),
                                             reads=[pbh, bP], writes=[zb])
                                        if first_in_slot:
                                            S.op(DVE, lambda: nc.vector.tensor_scalar(out=z[:, 0:2], in0=z[:, 0:2], scalar1=FLG[:, sl:sl + 1], scalar2=None, op0=ALU.mult),
                                                 reads=[zb, bC], writes=[zb])
                                    if i == NTILE - 1:
                                        S.op(DVE, lambda: nc.vector.memset(z[:, 514:515], 0.0), writes=[zb])
                                    else:
                                        S.op(DVE, lambda: nc.vector.tensor_scalar(out=z[:, 514:515], in0=psh[:, 2:3], scalar1=bcol, scalar2=None, op0=ALU.add),
                                             reads=[pbh, bP], writes=[zb])
                                        if last_in_slot:
                                            S.op(DVE, lambda: nc.vector.tensor_scalar(out=z[:, 514:515], in0=z[:, 514:515], scalar1=FLG[:, sl + 1:sl + 2], scalar2=None,
                                                                                      op0=ALU.mult), reads=[zb, bC], writes=[zb])
                                    xa, xab = xa_r.get()
                                    S.op(DVE, lambda: nc.vector.tensor_scalar(out=xa[:], in0=z[:, 0:512], scalar1=cw[:, 0, c8:c8 + 1], scalar2=cb[:, c8:c8 + 1],
                                                                              op0=ALU.mult, op1=ALU.add), reads=[zb, bP], writes=[xab])
                                    for j in range(1, 4):
                                        S.op(DVE,
                                             lambda j=j: nc.vector.scalar_tensor_tensor(out=xa[:], in0=z[:, j:j + 512], scalar=cw[:, j, c8:c8 + 1],
                                                                                                               in1=xa[:], op0=ALU.mult, op1=ALU.add),
                                             reads=[zb, xab, bP], writes=[xab])
                                    xb16, xb16b = xab_r.get()
                                    S.op(POOL, lambda: nc.gpsimd.tensor_copy(xb16[:], xa[:]), reads=[xab], writes=[xb16b])
                                    def gates(c8=c8, xa=xa, xab=xab, xb16=xb16, xb16b=xb16b):
                                        tl = []
                                        for d in range(2):
                                            psr, pbr = psum()
                                            mm_group(psr, pbr, psr[:], [(BD[:, 2 * d, c8, :], xb16[:], [bP, xb16b])])
                                            psi, pbi = psum()
                                            mm_group(psi, pbi, psi[:], [(BD[:, 2 * d + 1, c8, :], xb16[:], [bP, xb16b])])
                                            tl.append((psr, pbr, psi, pbi) + f32r.get() + f32r.get() + f32r.get())
                                        for d in range(2):
                                            psr, pbr, psi, pbi, rt, rtb, it, itb, at_, atb = tl[d]
                                            S.op(ACT, lambda: nc.scalar.activation(out=rt[:], in_=psr[:], func=AF.Sigmoid, bias=bat[:, d, c8:c8 + 1]), reads=[pbr, bP], writes=[rtb])
                                            S.op(ACT, lambda: nc.scalar.activation(out=it[:], in_=psi[:], func=AF.Sigmoid, bias=bxt[:, d, c8:c8 + 1]), reads=[pbi, bP], writes=[itb])
                                            S.op(POOL, lambda: nc.gpsimd.tensor_tensor(out=it[:], in0=it[:], in1=xa[:], op=ALU.mult), reads=[itb, xab], writes=[itb])
                                        for d in range(2):
                                            psr, pbr, psi, pbi, rt, rtb, it, itb, at_, atb = tl[d]
                                            S.op(ACT, lambda: nc.scalar.activation(out=at_[:], in_=rt[:], func=AF.Exp, scale=st1[:, d, c8:c8 + 1]), reads=[rtb, bP], writes=[atb])
                                            S.dma(POOL, A_s[d][c8 * 128:(c8 + 1) * 128, t0:t0 + NT], at_[:], atb, reads=[atb])
                                            S.op(DVE, lambda: nc.vector.tensor_tensor(out=rt[:], in0=at_[:], in1=at_[:], op=ALU.mult), reads=[atb], writes=[rtb])
                                        for d in range(2):
                                            psr, pbr, psi, pbi, rt, rtb, it, itb, at_, atb = tl[d]
                                            S.op(ACT, lambda: nc.scalar.activation(out=rt[:], in_=rt[:], func=AF.Sqrt, scale=-1.0, bias=1.0), reads=[rtb], writes=[rtb])
                                            S.op(DVE, lambda: nc.vector.tensor_tensor(out=it[:], in0=it[:], in1=rt[:], op=ALU.mult), reads=[itb, rtb], writes=[itb])
                                            S.dma(POOL, U_s[d][c8 * 128:(c8 + 1) * 128, t0:t0 + NT], it[:], itb, reads=[itb])
                                    prev = list(pending)
                                    del pending[:]
                                    pending.append(gates)
                                    for f_ in prev:
                                        f_()
                                yield from fm_group_gen(O_RX + c8_ * 128, 1, ev_rx, halo=True)

                        def simple(c0, groups, func, dst):
                            for g in groups:
                                def ev(m, ps, pb, psh, pbh, g=g):
                                    bi = blk_idx[c0] + g * 4 + m
                                    o, ob = f32r.get()
                                    S.op(ACT, lambda: nc.scalar.activation(out=o[:], in_=ps[:], func=func, bias=binfm[:, bi:bi + 1]), reads=[pb, bP], writes=[ob])
                                    r0_ = (g * 4 + m) * 128
                                    S.dma(POOL, dst[r0_:r0_ + 128, t0:t0 + NT], o[:], ob, reads=[ob])
                                fm_group(c0 + g * 512, 4, ev)
                        pending = []
                        do_g()
                        simple(O_RG, [0, 1], AF.Gelu_apprx_tanh, GRG)
                        do_logits()
                        fm_group(O_Q, 4, ev_q)
                        fm_group(O_K, 4, ev_k)
                        do_ktv()
                        do_cumsum()
                        simple(O_OG, [0, 1], AF.Silu, SOG)
                        do_kend()
                        bulk = [(O_MA, g_, SMA) for g_ in range(4)] + [(O_MB, g_, SMB) for g_ in range(4)]
                        for k_, _m in enumerate(do_rx()):
                            c0_, g_, dst_ = bulk[k_]
                            simple(c0_, [g_], AF.Sigmoid, dst_)
                        for f_ in pending:
                            f_()
                        del pending[:]
                    S.barrier()

                with ExitStack() as p2:
                    AtD = [sb(p2, "At%d" % d, (128, T), F32) for d in range(2)]; AbD = [Buf("At%d" % d) for d in range(2)]
                    UtD = [sb(p2, "Ut%d" % d, (128, T), F32) for d in range(2)]; UbD = [Buf("Ut%d" % d) for d in range(2)]
                    Hf = sb(p2, "Hf", (128, T), F32); Hfb = Buf("Hf")
                    Hb = sb(p2, "Hb", (128, T), F32); Hbb = Buf("Hb")
                    Ho = sb(p2, "Ho", (128, T // 4), BF16); Hob = Buf("Ho")
                    Q4 = [slice(q4 * (T // 4), (q4 + 1) * (T // 4)) for q4 in range(4)]
                    for c8 in range(8):
                        rows = slice(c8 * 128, (c8 + 1) * 128)
                        for (dst, dbuf, srct) in ((AtD[0], AbD[0], A_s[0]), (AtD[1], AbD[1], A_s[1]), (UtD[1], UbD[1], U_s[1]), (UtD[0], UbD[0], U_s[0])):
                            for cs in Q4:
                                S.dma(SP, dst[:, cs], srct[rows, cs], dbuf, writes=[dbuf])
                        for d in range(2):
                            Hd, Hdb = (Hf, Hfb) if d == 0 else (Hb, Hbb)
                            At, Ab, Ut, Ub = AtD[d], AbD[d], UtD[d], UbD[d]
                            order = range(NS) if d == 0 else range(NS - 1, -1, -1)
                            for n, s in enumerate(order):
                                lo, hi = s * SL, (s + 1) * SL
                                if d == 0:
                                    if n > 0:
                                        S.op(DVE, lambda: nc.vector.tensor_scalar(out=At[:, lo:lo + 1], in0=At[:, lo:lo + 1], scalar1=FLG[:, s:s + 1], scalar2=None,
                                                                                  op0=ALU.mult), reads=[Ab, bC], writes=[Ab])
                                    init = Hd[:, lo - 1:lo] if n > 0 else 0.0
                                    S.op(DVE, lambda: nc.vector.tensor_tensor_scan(out=Hd[:, lo:hi], data0=At[:, lo:hi], data1=Ut[:, lo:hi],
                                                                                   initial=init, op0=ALU.mult, op1=ALU.add),
                                         reads=[Ab, Ub, Hdb], writes=[Hdb])
                                else:
                                    if n > 0:
                                        S.op(DVE, lambda: nc.vector.tensor_scalar(out=At[:, hi - 1:hi], in0=At[:, hi - 1:hi], scalar1=FLG[:, s + 1:s + 2], scalar2=None,
                                                                                  op0=ALU.mult), reads=[Ab, bC], writes=[Ab])
                                    init = Hd[:, hi:hi + 1] if n > 0 else 0.0
                                    S.op(DVE, lambda: nc.vector.tensor_tensor_scan(out=Hd[:, lo:hi][:, ::-1], data0=At[:, lo:hi][:, ::-1],
                                                                                   data1=Ut[:, lo:hi][:, ::-1],
                                                                                   initial=init, op0=ALU.mult, op1=ALU.add),
                                         reads=[Ab, Ub, Hdb], writes=[Hdb])
                            if d == 0:
                                for cs in Q4:
                                    S.dma(SP, UtD[0][:, cs], GRG[rows, cs], UbD[0], writes=[UbD[0]])
                        for q4 in range(4):
                            cs = Q4[q4]
                            S.op(POOL, lambda: nc.gpsimd.tensor_tensor(out=Hf[:, cs], in0=Hf[:, cs], in1=Hb[:, cs], op=ALU.add), reads=[Hfb, Hbb], writes=[Hfb])
                            S.op(DVE, lambda: nc.vector.tensor_tensor(out=Ho[:], in0=Hf[:, cs], in1=UtD[0][:, cs], op=ALU.mult), reads=[Hfb, UbD[0]], writes=[Hob])
                            S.dma(POOL, HG[rows, cs], Ho[:], Hob, reads=[Hob])
                    S.barrier()

                if l + 1 < DEPTH:
                    S.protected.add(bWa[l + 1])
                    S.protected.add(bWb[l + 1])
                    cast_queue.extend(cast_jobs(l + 1, 0) + cast_jobs(l + 1, 1))
                with ExitStack() as p2:
                    qd_r = Ring(p2, "gq", (128, 4, NT), BF16, 2)
                    kd_r = Ring(p2, "gk", (128, 4, NT), BF16, 2)
                    ke_r = Ring(p2, "ge", (128, 4, 512), BF16, 2)
                    v_r = Ring(p2, "gv", (128, 4, 1024), BF16, 2)
                    o_r = Ring(p2, "go", (128, 8, NT), F32, 2)
                    of_r = Ring(p2, "gof", (128, 8, NT), F32, 2)
                    att_r = Ring(p2, "gatt", (128, 512), BF16, 3)
                    dec_t = sb(p2, "gdec", (128, 4, NCH), F32); dec_b = Buf("gdec")
                    Sf = sb(p2, "Sf", (128, 4, 256), F32); Sfb = Buf("Sf")
                    Sb_ = sb(p2, "Sb", (128, 4, 256), BF16); Sbb = Buf("Sb")
                    MASKF4 = sb(p2, "MASKF4", (128, 512), F32)
                    MASKB4 = sb(p2, "MASKB4", (128, 512), F32)
                    bM4 = Buf("mask4")
                    for h in range(4):
                        S.op(DVE, lambda: nc.vector.tensor_copy(MASKF4[:, h * 128:(h + 1) * 128], MASKF[:]), reads=[bC], writes=[bM4])
                        S.op(DVE, lambda: nc.vector.tensor_copy(MASKB4[:, h * 128:(h + 1) * 128], MASKB[:]), reads=[bC], writes=[bM4])
                    for d in range(2):
                        MASK4 = MASKF4 if d == 0 else MASKB4
                        S.dma(SP, dec_t[:], DEC_s[d].rearrange("(h p) n -> p h n", p=128), dec_b, writes=[dec_b])
                        S.op(DVE, lambda: nc.vector.memset(Sf[:], 0.0), writes=[Sfb])
                        S.op(POOL, lambda: nc.gpsimd.memset(Sb_[:], 0.0), writes=[Sbb])
                        tiles = list(range(NTILE)) if d == 0 else list(range(NTILE - 1, -1, -1))
                        chunks = list(range(4)) if d == 0 else list(range(3, -1, -1))
                        steps = [(i, c) for i in tiles for c in chunks]
                        tdata = {}

                        def load_tile(i):
                            t0 = i * NT
                            qt, qb = qd_r.get(); kt, kb = kd_r.get(); et, eb = ke_r.get(); vt, vb = v_r.get()
                            S.dma(SP, qt[:], QD_s[d].rearrange("(h p) t -> p h t", p=128)[:, :, t0:t0 + NT], qb, writes=[qb])
                            S.dma(SP, kt[:], KD_s[d].rearrange("(h p) t -> p h t", p=128)[:, :, t0:t0 + NT], kb, writes=[kb])
                            S.dma(SP, et[:], KE_s[d].rearrange("(c p) n -> p c n", p=128)[:, i * 4:(i + 1) * 4, :], eb, writes=[eb])
                            S.dma(SP, vt[:], V_s.rearrange("(c p) n -> p c n", p=128)[:, i * 4:(i + 1) * 4, :], vb, writes=[vb])
                            ot, ob = o_r.get()
                            oft = ofb = None
                            if d == 1:
                                oft, ofb = of_r.get()
                                S.dma(SP, oft[:], O1.rearrange("(j p) t -> p j t", p=128)[:, :, t0:t0 + NT], ofb, writes=[ofb])
                            tdata[i] = (qt, qb, kt, kb, et, eb, vt, vb, ot, ob, oft, ofb)

                        stA = {}

                        def stage_A(n):
                            i, c = steps[n]
                            if i not in tdata:
                                load_tile(i)
                            qt, qb, kt, kb, et, eb, vt, vb = tdata[i][:8]
                            cs = slice(c * 128, (c + 1) * 128)
                            psa, pba = banks[n % 2], bbank[n % 2]
                            for h in range(4):
                                mm_group(psa, pba, psa[:, h * 128:(h + 1) * 128], [(kt[:, h, cs], qt[:, h, cs], [kb, qb])])
                            kb0 = 2 + 2 * (n % 2)
                            kvb = [(banks[kb0], bbank[kb0]), (banks[kb0 + 1], bbank[kb0 + 1])]
                            for h in range(4):
                                psk, pbk = kvb[h // 2]
                                mm_group(psk, pbk, psk[:, (h % 2) * 256:(h % 2 + 1) * 256],
                                         [(et[:, c, h * 128:(h + 1) * 128], vt[:, c, h * 256:(h + 1) * 256], [eb, vb])])
                            stA[n] = (psa, pba, kvb)

                        stage_A(0)
                        for n in range(len(steps)):
                            i, c = steps[n]
                            qt, qb, kt, kb, et, eb, vt, vb, ot, ob, oft, ofb = tdata[i]
                            psa, pba, kvb = stA.pop(n)
                            gch = i * 4 + c
                            cs = slice(c * 128, (c + 1) * 128)
                            tok = gch * 128
                            am, amb = att_r.get()
                            S.op(DVE, lambda: nc.vector.tensor_tensor(out=am[:], in0=psa[:], in1=MASK4[:], op=ALU.mult), reads=[pba, bM4], writes=[amb])
                            if n + 1 < len(steps):
                                stage_A(n + 1)
                            if d == 0 and tok % SL == 0 and tok > 0:
                                fl = FLG[:, tok // SL:tok // SL + 1]
                            elif d == 1 and (tok + 128) % SL == 0 and tok + 128 < T:
                                fl = FLG[:, (tok + 128) // SL:(tok + 128) // SL + 1]
                            else:
                                fl = None
                            if fl is not None:
                                S.op(DVE, lambda: nc.vector.tensor_scalar(out=Sf[:], in0=Sf[:], scalar1=fl, scalar2=None, op0=ALU.mult), reads=[Sfb, bC], writes=[Sfb])
                                S.op(ACT, lambda: nc.scalar.activation(out=Sb_[:], in_=Sf[:], func=AF.Copy), reads=[Sfb], writes=[Sbb])
                            pso = [(banks[6], bbank[6]), (banks[7], bbank[7])]
                            for j in range(2):
                                ps_, pb_ = pso[j]
                                for h in range(4):
                                    mm_group(ps_, pb_, ps_[:, h * 128:(h + 1) * 128],
                                             [(vt[:, c, h * 256 + j * 128:h * 256 + (j + 1) * 128], am[:, h * 128:(h + 1) * 128], [vb, amb]),
                                              (Sb_[:, h, j * 128:(j + 1) * 128], qt[:, h, cs], [Sbb, qb])])
                            for j in range(2):
                                ps_, pb_ = pso[j]
                                pv = ps_[:].rearrange("p (h c) -> p h c", h=4)
                                if d == 0:
                                    S.op(ACT, lambda: nc.scalar.activation(out=ot[:, j::2, cs], in_=pv, func=AF.Copy), reads=[pb_], writes=[ob])
                                else:
                                    S.op(DVE, lambda: nc.vector.tensor_tensor(out=ot[:, j::2, cs], in0=pv, in1=oft[:, j::2, cs], op=ALU.add),
                                         reads=[pb_, ofb], writes=[ob])
                            for h in range(4):
                                psk, pbk = kvb[h // 2]
                                S.op(DVE, lambda: nc.vector.scalar_tensor_tensor(out=Sf[:, h, :], in0=Sf[:, h, :], scalar=dec_t[:, h, gch:gch + 1],
                                                                                 in1=psk[:, (h % 2) * 256:(h % 2 + 1) * 256], op0=ALU.mult, op1=ALU.add),
                                     reads=[Sfb, dec_b, pbk], writes=[Sfb])
                            S.op(ACT, lambda: nc.scalar.activation(out=Sb_[:], in_=Sf[:], func=AF.Copy), reads=[Sfb], writes=[Sbb])
                            cast_some(1)
                            if c == chunks[-1]:
                                dst = O1 if d == 0 else O2
                                S.dma(POOL, dst.rearrange("(j p) t -> p j t", p=128)[:, :, i * NT:(i + 1) * NT], ot[:], ob, reads=[ob])
                                del tdata[i]
                        if d == 1:
                            cast_some(10 ** 6)
                        S.barrier()

                if l == 0:
                    S.protected.discard(bWb[0])
                    S.barrier()
                with ExitStack() as p3:
                    wp_ring = Ring(p3, "w3p", (128, 8, 512), BF16, 4)
                    wo_ring = Ring(p3, "w3o", (128, 16, 512), BF16, 2)
                    wseq_p, wseq_o = {}, {}

                    def seq_p(i):
                        if i not in wseq_p:
                            specs = []
                            for g in range(4):
                                specs.append((wq_pa[l], 0, 8, g * 512, 512))
                                specs.append((wq_pb[l], 0, 8, g * 512, 512))
                            wseq_p[i] = WSeq(wp_ring, specs, 2)
                        return wseq_p[i]

                    def seq_o(i):
                        if i not in wseq_o:
                            wseq_o[i] = WSeq(wo_ring, [(wq_out[l], 0, 16, g * 512, 512) for g in range(4)], 1)
                        return wseq_o[i]
                    hg_r = Ring(p3, "hg", (128, 8, NT), BF16, 2)
                    ogt2 = [sb(p3, "ogt%d" % k, (128, 8, NT), BF16) for k in range(2)]; ogb2 = [Buf("ogt%d" % k) for k in range(2)]
                    mg2 = [sb(p3, "mg%d" % k, (128, 16, NT), BF16) for k in range(2)]; mgb2 = [Buf("mg%d" % k) for k in range(2)]
                    r3 = sb(p3, "r3", (128, 16, NT), F32); r3bs = [Buf("r3_%d" % k) for k in range(16)]
                    pc_r = Ring(p3, "pc3", (128, NT), F32, 6)
                    o2_r = Ring(p3, "o23", (128, 2, NT), F32, 2)
                    f32r = Ring(p3, "f3", (128, NT), F32, 2)
                    sqr = Ring(p3, "sq3", (128, NT), BF16, 4)
                    st = [sb(p3, "st3_%d" % i, (128, NT), F32) for i in range(4)]
                    st_buf = Buf("st3")
                    obf_r = Ring(p3, "obf3", (128, NT), BF16, 2)
                    hgs = {}

                    def rms_head(i, h):
                        tsl = slice(i * NT, (i + 1) * NT)
                        ogt, ogb = ogt2[i % 2], ogb2[i % 2]
                        if h == 0:
                            hgt, hgb = hg_r.get()
                            hgs[i] = (hgt, hgb)
                            S.dma(SP, hgt[:], HG.rearrange("(c p) t -> p c t", p=128)[:, :, tsl], hgb, writes=[hgb])
                        if True:
                            o2, o2b = o2_r.get()
                            S.dma(SP, o2[:], O2.rearrange("(j p) t -> p j t", p=128)[:, 2 * h:2 * h + 2, tsl], o2b, writes=[o2b])
                            psn, pbn = psum()
                            for j in range(2):
                                sq, sqb = sqr.get()
                                S.op(ACT, lambda: nc.scalar.activation(out=sq[:], in_=o2[:, j, :], func=AF.Square), reads=[o2b], writes=[sqb])
                                S.op(PE, lambda: nc.tensor.matmul(psn[:], ONESV_B[:], sq[:], start=(j == 0), stop=(j == 1)), reads=[sqb, bC], writes=[pbn], signal=True)
                            rs, rsb = f32r.get()
                            S.op(ACT, lambda: nc.scalar.activation(out=rs[:], in_=psn[:], func=AF.Sqrt, bias=EPSC[:, 0:1]), reads=[pbn, bC], writes=[rsb])
                            S.op(DVE, lambda: nc.vector.reciprocal(out=rs[:], in_=rs[:]), reads=[rsb], writes=[rsb])
                            for j in range(2):
                                S.op(DVE, lambda: nc.vector.scalar_tensor_tensor(out=o2[:, j, :], in0=o2[:, j, :], scalar=nwt[:, j:j + 1], in1=rs[:],
                                                                                 op0=ALU.mult, op1=ALU.mult), reads=[o2b, rsb, bP], writes=[o2b])
                                pc, pcb = pc_r.get()
                                rr = (2 * h + j) * 128
                                S.dma(SP, pc[:], SOG[rr:rr + 128, tsl], pcb, writes=[pcb])
                                S.op(POOL, lambda: nc.gpsimd.tensor_tensor(out=ogt[:, 2 * h + j, :], in0=o2[:, j, :], in1=pc[:], op=ALU.mult),
                                     reads=[o2b, pcb], writes=[ogb])

                    def proj_stage(i, hook=None):
                        tsl = slice(i * NT, (i + 1) * NT)
                        ogt, ogb = ogt2[i % 2], ogb2[i % 2]
                        mg, mgb = mg2[i % 2], mgb2[i % 2]
                        hgt, hgb = hgs.pop(i)
                        for g in range(4):
                            wa, wab = seq_p(i).get(2 * g)
                            wb_, wbb = seq_p(i).get(2 * g + 1)
                            for m in range(4):
                                mb = g * 4 + m
                                psa, pba = psum()
                                mm_group(psa, pba, psa[:], [(wa[:, kc, m * 128:(m + 1) * 128], hgt[:, kc, :], [wab, hgb]) for kc in range(8)])
                                psb, pbb = psum()
                                mm_group(psb, pbb, psb[:], [(wb_[:, kc, m * 128:(m + 1) * 128], ogt[:, kc, :], [wbb, ogb]) for kc in range(8)])
                                pa, pab = pc_r.get()
                                S.dma(SP, pa[:], SMA[mb * 128:(mb + 1) * 128, tsl], pab, writes=[pab])
                                pb2, pb2b = pc_r.get()
                                S.dma(SP, pb2[:], SMB[mb * 128:(mb + 1) * 128, tsl], pb2b, writes=[pb2b])
                                S.op(DVE, lambda: nc.vector.tensor_tensor(out=pa[:], in0=psa[:], in1=pa[:], op=ALU.mult), reads=[pba, pab], writes=[pab])
                                S.op(DVE, lambda: nc.vector.tensor_tensor(out=pb2[:], in0=psb[:], in1=pb2[:], op=ALU.mult), reads=[pbb, pb2b], writes=[pb2b])
                                S.op(POOL, lambda: nc.gpsimd.tensor_tensor(out=mg[:, mb, :], in0=pa[:], in1=pb2[:], op=ALU.add),
                                     reads=[pab, pb2b], writes=[mgb])
                                if hook is not None:
                                    hook(mb)

                    def wout_group(i, g):
                        tsl = slice(i * NT, (i + 1) * NT)
                        mg, mgb = mg2[i % 2], mgb2[i % 2]
                        if True:
                            wo, wob = seq_o(i).get(g)
                            for m in range(4):
                                mb = g * 4 + m
                                ps, pb = psum()
                                mm_group(ps, pb, ps[:], [(wo[:, kc, m * 128:(m + 1) * 128], mg[:, kc, :], [wob, mgb]) for kc in range(16)])
                                pc, pcb = pc_r.get()
                                S.dma(SP, pc[:], xres[mb * 128:(mb + 1) * 128, tsl], pcb, writes=[pcb])
                                S.op(DVE, lambda: nc.vector.scalar_tensor_tensor(out=r3[:, mb, :], in0=pc[:], scalar=ALPHA, in1=ps[:], op0=ALU.mult, op1=ALU.add),
                                     reads=[pcb, pb], writes=[r3bs[mb]])

                    def ln_stage(i):
                        t0 = i * NT

                        def out_cb(kc, ob_, obb):
                            S.dma(POOL, x1res[kc * 128:(kc + 1) * 128, t0:t0 + NT], r3[:, kc, :], r3bs[kc], reads=[r3bs[kc]])
                            S.dma(POOL, x1bf[kc * 128:(kc + 1) * 128, 1 + t0:1 + t0 + NT], ob_[:], obb, reads=[obb])
                        ln_fm((sqr, st), r3, r3bs, gm_t, bm_t, out_cb, obf_r, do_norm=False)
                        return lambda kc: ln_norm(kc, st, st_buf, r3, r3bs, gm_t, bm_t, out_cb, obf_r)

                    seq_p(0).prime(2)
                    for h in range(4):
                        rms_head(0, h)
                    proj_stage(0)
                    seq_o(0).prime(1)
                    for i in range(NTILE):
                        for g in range(4):
                            if i + 1 < NTILE:
                                rms_head(i + 1, g)
                            wout_group(i, g)
                        if i + 1 < NTILE:
                            seq_p(i + 1).prime(2)
                        normf = ln_stage(i)
                        if i + 1 < NTILE:
                            proj_stage(i + 1, hook=normf)
                            seq_o(i + 1).prime(1)
                        else:
                            for kc in range(16):
                                normf(kc)
                    S.barrier()

                with ExitStack() as p4:
                    ws = WStream(p4, "w4", 4, 256)
                    xT = sb(p4, "xT4", (128, 16, 516), BF16); xTb = Buf("xT4")
                    hdn = sb(p4, "hdn", (128, 48, NT), BF16); hdb = Buf("hdn")
                    r4 = sb(p4, "r4", (128, 16, NT), F32); r4bs = [Buf("r4_%d" % k) for k in range(16)]
                    ug_r = Ring(p4, "ug", (128, 516), F32, 2)
                    f32r = Ring(p4, "f4", (128, NT), F32, 3)
                    pc_r = Ring(p4, "pc4", (128, NT), F32, 3)
                    sqr = Ring(p4, "sq4", (128, NT), BF16, 4)
                    st = [sb(p4, "st4_%d" % i, (128, NT), F32) for i in range(4)]
                    st_buf = Buf("st4")
                    obf_r = Ring(p4, "obf4", (128, NT), BF16, 2)
                    yst_r = Ring(p4, "yst", (128, 512), F32, 3)
                    Wu = wq_up[l]
                    pend4 = []
                    for i in range(NTILE):
                        t0 = i * NT
                        tsl = slice(t0, t0 + NT)
                        sl = i // TPS
                        first_in_slot = (i % TPS == 0)
                        last_in_slot = (i % TPS == TPS - 1)
                        S.dma(SP, xT[:, :, 0:514], x1bf.rearrange("(kc p) t -> p kc t", p=128)[:, :, t0:t0 + 514], xTb, writes=[xTb])
                        for g in range(24):
                            if pend4 and g < 16:
                                pend4[0](g)
                                if g == 15:
                                    pend4[1]()
                                    del pend4[:]
                            wg_, wgb = ws.load(Wu, 0, 16, g * 256, 256)
                            wv_, wvb = ws.load(Wu, 0, 16, D_FF + g * 256, 256)
                            for m in range(2):
                                hb_ = g * 2 + m
                                ps, pb = psum()
                                mm_group(ps, pb, ps[:], [(wg_[:, kc, m * 128:(m + 1) * 128], xT[:, kc, 1:513], [wgb, xTb]) for kc in range(16)])
                                psh, pbh = psum()
                                mm_group(psh, pbh, psh[:, 0:1], [(wg_[:, kc, m * 128:(m + 1) * 128], xT[:, kc, 0:1], [wgb, xTb]) for kc in range(16)])
                                mm_group(psh, pbh, psh[:, 1:2], [(wg_[:, kc, m * 128:(m + 1) * 128], xT[:, kc, 513:514], [wgb, xTb]) for kc in range(16)])
                                psv, pbv = psum()
                                mm_group(psv, pbv, psv[:], [(wv_[:, kc, m * 128:(m + 1) * 128], xT[:, kc, 1:513], [wvb, xTb]) for kc in range(16)])
                                u, ub = ug_r.get()
                                S.op(ACT, lambda u=u, ps=ps: nc.scalar.activation(out=u[:, 1:513], in_=ps[:], func=AF.Copy), reads=[pb], writes=[ub])
                                if i == 0:
                                    S.op(DVE, lambda u=u: nc.vector.memset(u[:, 0:1], 0.0), writes=[ub])
                                elif first_in_slot:
                                    S.op(DVE, lambda u=u, psh=psh: nc.vector.tensor_scalar(out=u[:, 0:1], in0=psh[:, 0:1], scalar1=FLG[:, sl:sl + 1], scalar2=None, op0=ALU.mult),
                                         reads=[pbh, bC], writes=[ub])
                                else:
                                    S.op(DVE, lambda u=u, psh=psh: nc.vector.tensor_copy(u[:, 0:1], psh[:, 0:1]), reads=[pbh], writes=[ub])
                                if i == NTILE - 1:
                                    S.op(DVE, lambda u=u: nc.vector.memset(u[:, 513:514], 0.0), writes=[ub])
                                elif last_in_slot:
                                    S.op(DVE, lambda u=u, psh=psh: nc.vector.tensor_scalar(out=u[:, 513:514], in0=psh[:, 1:2], scalar1=FLG[:, sl + 1:sl + 2], scalar2=None,
                                                                                          op0=ALU.mult), reads=[pbh, bC], writes=[ub])
                                else:
                                    S.op(DVE, lambda u=u, psh=psh: nc.vector.tensor_copy(u[:, 513:514], psh[:, 1:2]), reads=[pbh], writes=[ub])
                                acc, accb = f32r.get()
                                S.op(DVE, lambda u=u, acc=acc, hb_=hb_: nc.vector.tensor_scalar(out=acc[:], in0=u[:, 0:512], scalar1=cfw[:, 0, hb_:hb_ + 1], scalar2=cfb[:, hb_:hb_ + 1],
                                                                                             op0=ALU.mult, op1=ALU.add), reads=[ub, bP], writes=[accb])
                                S.op(DVE, lambda u=u, acc=acc, hb_=hb_: nc.vector.scalar_tensor_tensor(out=acc[:], in0=u[:, 1:513], scalar=cfw[:, 1, hb_:hb_ + 1], in1=acc[:],
                                                                                                     op0=ALU.mult, op1=ALU.add), reads=[ub, accb, bP], writes=[accb])
                                S.op(DVE, lambda u=u, acc=acc, hb_=hb_: nc.vector.scalar_tensor_tensor(out=acc[:], in0=u[:, 2:514], scalar=cfw[:, 2, hb_:hb_ + 1], in1=acc[:],
                                                                                                     op0=ALU.mult, op1=ALU.add), reads=[ub, accb, bP], writes=[accb])
                                S.op(ACT, lambda acc=acc: nc.scalar.activation(out=acc[:], in_=acc[:], func=AF.Gelu_apprx_tanh), reads=[accb], writes=[accb])
                                S.op(DVE, lambda acc=acc, psv=psv, hb_=hb_: nc.vector.tensor_tensor(out=hdn[:, hb_, :], in0=psv[:], in1=acc[:], op=ALU.mult),
                                     reads=[pbv, accb], writes=[hdb])
                        for g in range(8):
                            pss = [psum() for _ in range(2)]
                            for ks in range(3):
                                wd, wdb = ws.load(wq_dn[l], ks * 16, 16, g * 256, 256)
                                for m in range(2):
                                    ps, pb = pss[m]
                                    for kc in range(16):
                                        S.op(PE, lambda ps=ps, wd=wd, m=m, kc=kc, ks=ks: nc.tensor.matmul(ps[:], wd[:, kc, m * 128:(m + 1) * 128], hdn[:, ks * 16 + kc, :],
                                                                                                      start=(ks == 0 and kc == 0), stop=(ks == 2 and kc == 15)),
                                             reads=[wdb, hdb], writes=[pb], signal=(kc == 15))
                            for m in range(2):
                                mb = g * 2 + m
                                ps, pb = pss[m]
                                pc, pcb = pc_r.get()
                                S.dma(SP, pc[:], x1res[mb * 128:(mb + 1) * 128, tsl], pcb, writes=[pcb])
                                S.op(DVE, lambda pc=pc, ps=ps, mb=mb: nc.vector.scalar_tensor_tensor(out=r4[:, mb, :], in0=pc[:], scalar=ALPHA, in1=ps[:], op0=ALU.mult, op1=ALU.add),
                                     reads=[pcb, pb], writes=[r4bs[mb]])

                        if not last:
                            def out_cb(kc, ob_, obb, t0=t0):
                                S.dma(POOL, xres[kc * 128:(kc + 1) * 128, t0:t0 + NT], r4[:, kc, :], r4bs[kc], reads=[r4bs[kc]])
                                S.dma(POOL, xbf[kc * 128:(kc + 1) * 128, 2 + t0:2 + t0 + NT], ob_[:], obb, reads=[obb])
                            ln_fm((sqr, st), r4, r4bs, gf_t, bf_t, out_cb, obf_r, do_norm=False)
                            normf = (lambda kc, out_cb=out_cb: ln_norm(kc, st, st_buf, r4, r4bs, gf_t, bf_t, out_cb, obf_r))
                            finf = (lambda: None)
                        else:
                            ln_fm((sqr, st), r4, r4bs, gf_t, bf_t, None, None, do_norm=False)
                            normf = (lambda kc: ln_norm(kc, st, st_buf, r4, r4bs, gf_t, bf_t, lambda kc_, ob_, obb: None, None))

                            def finf(t0=t0):
                                for c in range(4):
                                    for k4 in range(4):
                                        ps, pb = psum()
                                        for k in range(4):
                                            kc = k4 * 4 + k
                                            S.op(PE, lambda: nc.tensor.transpose(ps[:, k * 128:(k + 1) * 128], r4[:, kc, c * 128:(c + 1) * 128], IDENT[:]),
                                                 reads=[r4bs[kc], bC], writes=[pb], signal=(k == 3))
                                        ys, ysb = yst_r.get()
                                        if k4 % 2 == 0:
                                            S.op(ACT, lambda: nc.scalar.activation(out=ys[:], in_=ps[:], func=AF.Copy), reads=[pb], writes=[ysb])
                                        else:
                                            S.op(DVE, lambda: nc.vector.tensor_copy(ys[:], ps[:]), reads=[pb], writes=[ysb])
                                        S.dma(POOL, y_out[t0 + c * 128:t0 + (c + 1) * 128, k4 * 512:(k4 + 1) * 512], ys[:], ysb, reads=[ysb])
                        if i + 1 < NTILE:
                            pend4[:] = [normf, finf]
                        else:
                            for kc in range(16):
                                normf(kc)
                            finf()
                    S.barrier()
        S.barrier()
    return nc


_W_NAMES = ["ln_in_g", "ln_in_b", "w_in", "b_in", "conv_rg_w", "conv_rg_b", "rg_wa", "rg_ba", "rg_wx", "rg_bx", "rg_lam",
            "gla_wg2", "gla_bg", "gla_norm_w", "w_proj_a", "w_proj_b", "w_out", "ln_mix_g", "ln_mix_b", "w_up",
            "conv_ff_w", "conv_ff_b", "w_down", "ln_ffn_g", "ln_ffn_b"]


def run_cores(core_x, core_flags, weights, NS, SL):
    nc = build_program(NS, SL)
    wd = {k: np.ascontiguousarray(np.asarray(weights[k], dtype=np.float32)) for k in _W_NAMES}
    in_maps = []
    for xc, fl in zip(core_x, core_flags):
        m = dict(wd)
        m["x"] = np.ascontiguousarray(xc, dtype=np.float32)
        m["flags"] = np.ascontiguousarray(np.broadcast_to(np.asarray(fl, dtype=np.float32)[None, :], (128, 4)))
        in_maps.append(m)
    res = run_bass_kernel_spmd(nc, in_maps, core_ids=list(range(len(core_x))))
    return [r["y"] for r in res.results]


def kernel(**inputs):
    xp = np.asarray(inputs["x_prompt"], dtype=np.float32)
    xs = np.asarray(inputs["x_sample"], dtype=np.float32)
    NS, SL = 4, 2048
    samp_of_core = [[0, 1, 2], [3, 4, 5], [6, 7, 8], [9, 10, 11], [12, 13], [14, 15]]
    core_x, core_flags = [], []
    for b in range(2):
        core_x.append(xp[b])
        core_flags.append([0.0, 1.0, 1.0, 1.0])
    for lst in samp_of_core:
        xc = np.zeros((NS * SL, D), np.float32)
        for j, s in enumerate(lst):
            xc[j * SL:(j + 1) * SL] = xs[s]
        core_x.append(xc)
        core_flags.append([0.0, 0.0, 0.0, 0.0])
    ys = run_cores(core_x, core_flags, inputs, NS, SL)
    y_prompt = np.stack([ys[0], ys[1]], axis=0).astype(np.float32)
    y_sample = np.zeros((16, SL, D), np.float32)
    for ci, lst in enumerate(samp_of_core):
        for j, s in enumerate(lst):
            y_sample[s] = ys[2 + ci][j * SL:(j + 1) * SL]
    return (y_prompt, y_sample)
```
